# Optimizing a Trainium2 kernel written in Bass

```python
import jax, jax.numpy as jnp
from jax import lax
import numpy as np

D_MODEL = 1024
BATCH = 2
SEQ = 8192
DEPTH = 1

D_PLE = 256
D_MIX = 2 * D_MODEL
ATT_HEADS = 16
ATT_KV_HEADS = 2
ATT_HEAD_DIM = 64
ATT_WIDTH = ATT_HEADS * ATT_HEAD_DIM
ATT_KV_WIDTH = ATT_KV_HEADS * ATT_HEAD_DIM
WINDOW = 128
BLOCK = 128
ROPE_DIM = ATT_HEAD_DIM // 4
ROPE_THETA = 500000.0
GLA_WIDTH = D_MIX - ATT_WIDTH
GLA_HEADS = 4
GLA_V_DIM = GLA_WIDTH // GLA_HEADS
GLA_K_DIM = GLA_V_DIM // 2
GLA_QK_WIDTH = GLA_HEADS * GLA_K_DIM
GLA_GATE_RANK = 16
GLA_GATE_NORMALIZER = 16.0
GLA_CHUNK = 64
EPS = 1e-6
SPLITS = (ATT_WIDTH, ATT_KV_WIDTH, ATT_KV_WIDTH,
          GLA_QK_WIDTH, GLA_QK_WIDTH, GLA_WIDTH,
          GLA_GATE_RANK,
          ATT_WIDTH, GLA_WIDTH)
D_IN_PROJ = sum(SPLITS)

kernel_name = 'hymba_swa_sink_gla_ple_block'


def rmsnorm(x, g):
    xf = x.astype(jnp.float32)
    y = xf * lax.rsqrt(jnp.mean(xf * xf, axis=-1, keepdims=True) + EPS)
    return (y * g.astype(jnp.float32)).astype(x.dtype)


def partial_rope(x, positions):
    half = ROPE_DIM // 2
    inv_freq = ROPE_THETA ** (-jnp.arange(0, ROPE_DIM, 2, dtype=jnp.float32) / ROPE_DIM)
    ang = positions.astype(jnp.float32)[..., None] * inv_freq
    cos = jnp.cos(ang)[:, :, None, :].astype(x.dtype)
    sin = jnp.sin(ang)[:, :, None, :].astype(x.dtype)
    x1 = x[..., :half]
    x2 = x[..., half:ROPE_DIM]
    return jnp.concatenate([x1 * cos - x2 * sin, x2 * cos + x1 * sin, x[..., ROPE_DIM:]], axis=-1)


def swa_sink_attention(q, k, v, sinks):
    B, S, H, dh = q.shape
    nb = S // BLOCK
    G = H // ATT_KV_HEADS
    qb = q.reshape(B, nb, BLOCK, ATT_KV_HEADS, G, dh)
    kb = k.reshape(B, nb, BLOCK, ATT_KV_HEADS, dh)
    vb = v.reshape(B, nb, BLOCK, ATT_KV_HEADS, dh)
    pad = jnp.zeros_like(kb[:, :1])
    kk = jnp.concatenate([jnp.concatenate([pad, kb[:, :-1]], axis=1), kb], axis=2)
    vv = jnp.concatenate([jnp.concatenate([pad, vb[:, :-1]], axis=1), vb], axis=2)
    s = jnp.einsum('bnqhgd,bnkhd->bnhgqk', qb, kk).astype(jnp.float32) * (dh ** -0.5)
    qi = jnp.arange(BLOCK)[:, None]
    kj = jnp.arange(2 * BLOCK)[None, :]
    diff = qi + BLOCK - kj
    band = (diff >= 0) & (diff < WINDOW)
    blk = jnp.arange(nb)[:, None, None]
    mask = band[None] & ((blk > 0) | (kj >= BLOCK)[None])
    s = jnp.where(mask[None, :, None, None], s, -jnp.inf)
    sink = sinks.astype(jnp.float32).reshape(ATT_KV_HEADS, G)[None, None, :, :, None, None]
    m = jnp.maximum(jnp.max(s, axis=-1, keepdims=True), sink)
    e = jnp.exp(s - m)
    pr = (e / (jnp.sum(e, axis=-1, keepdims=True) + jnp.exp(sink - m))).astype(v.dtype)
    o = jnp.einsum('bnhgqk,bnkhd->bnqhgd', pr, vv)
    return o.reshape(B, S, H * dh)


def gla_chunked(q, k, v, log_g):
    B, S, H, dk = q.shape
    dv = v.shape[-1]
    nc = S // GLA_CHUNK
    f = lambda t: t.astype(jnp.float32).reshape(B, nc, GLA_CHUNK, H, t.shape[-1])
    qf = f(q) * (dk ** -0.5)
    kf = f(k)
    vf = f(v)
    b = jnp.cumsum(f(log_g), axis=2)
    b_last = b[:, :, -1]
    q_dec = qf * jnp.exp(b)
    k_inv = kf * jnp.exp(-b)
    k_tail = kf * jnp.exp(b_last[:, :, None] - b)
    causal = jnp.tril(jnp.ones((GLA_CHUNK, GLA_CHUNK), dtype=bool))
    a = jnp.where(causal, jnp.einsum('bnihd,bnjhd->bnhij', q_dec, k_inv), 0.0)
    o_intra = jnp.einsum('bnhij,bnjhv->bnihv', a, vf)
    kv = jnp.einsum('bnjhd,bnjhv->bnhdv', k_tail, vf)
    decay = jnp.exp(b_last)

    def step(state, inp):
        dec, kvc = inp
        return dec[..., None] * state + kvc, state

    init = jnp.zeros((B, H, dk, dv), jnp.float32)
    _, s_prev = lax.scan(step, init, (jnp.moveaxis(decay, 1, 0), jnp.moveaxis(kv, 1, 0)))
    s_prev = jnp.moveaxis(s_prev, 0, 1)
    o_inter = jnp.einsum('bnihd,bnhdv->bnihv', q_dec, s_prev)
    return (o_intra + o_inter).reshape(B, S, H, dv)


def setup_inputs(seed: int = 0) -> dict:
    key = jax.random.key(seed)
    ks = jax.random.split(key, 16)
    nrm = lambda k, shape, scale: jax.random.normal(k, shape, jnp.float32) * scale
    x = nrm(ks[0], (BATCH, SEQ, D_MODEL), 1.0)
    p = nrm(ks[1], (DEPTH, BATCH, SEQ, D_PLE), 1.0)
    positions = jnp.broadcast_to(jnp.arange(SEQ, dtype=jnp.int32), (BATCH, SEQ))
    return {
        'x': x,
        'p': p,
        'positions': positions,
        'norm_mix': 1.0 + nrm(ks[2], (DEPTH, D_MODEL), 0.02),
        'w_in': nrm(ks[3], (DEPTH, D_MODEL, D_IN_PROJ), D_MODEL ** -0.5),
        'attn_sinks': nrm(ks[4], (DEPTH, ATT_HEADS), 0.5),
        'w_gate_up': nrm(ks[5], (DEPTH, GLA_GATE_RANK, GLA_QK_WIDTH), GLA_GATE_RANK ** -0.5),
        'b_gate': nrm(ks[6], (DEPTH, GLA_QK_WIDTH), 0.1),
        'gla_norm': 1.0 + nrm(ks[7], (DEPTH, GLA_V_DIM), 0.02),
        'w_out': nrm(ks[8], (DEPTH, D_MIX, D_MODEL), D_MIX ** -0.5),
        'ple_norm': 1.0 + nrm(ks[9], (DEPTH, D_MODEL), 0.02),
        'w_ple_gate': nrm(ks[10], (DEPTH, D_MODEL, D_MODEL), D_MODEL ** -0.5),
        'w_ple_proj': nrm(ks[11], (DEPTH, D_PLE, D_MODEL), D_PLE ** -0.5),
        'final_norm': 1.0 + nrm(ks[12], (D_MODEL,), 0.02),
    }


def reference(x, p, positions, norm_mix, w_in, attn_sinks, w_gate_up, b_gate, gla_norm,
              w_out, ple_norm, w_ple_gate, w_ple_proj, final_norm):
    B, S, _ = x.shape
    offsets = []
    acc = 0
    for n in SPLITS[:-1]:
        acc += n
        offsets.append(acc)
    h = x
    for i in range(DEPTH):
        u = rmsnorm(h, norm_mix[i])
        z = u @ w_in[i]
        aq, ak, av, gq, gk, gv, g_low, za, zg = jnp.split(z, offsets, axis=-1)
        aq = partial_rope(aq.reshape(B, S, ATT_HEADS, ATT_HEAD_DIM), positions)
        ak = partial_rope(ak.reshape(B, S, ATT_KV_HEADS, ATT_HEAD_DIM), positions)
        av = av.reshape(B, S, ATT_KV_HEADS, ATT_HEAD_DIM)
        y_att = swa_sink_attention(aq, ak, av, attn_sinks[i])
        log_g = jax.nn.log_sigmoid((g_low @ w_gate_up[i] + b_gate[i]).astype(jnp.float32)) / GLA_GATE_NORMALIZER
        o = gla_chunked(gq.reshape(B, S, GLA_HEADS, GLA_K_DIM),
                        gk.reshape(B, S, GLA_HEADS, GLA_K_DIM),
                        gv.reshape(B, S, GLA_HEADS, GLA_V_DIM),
                        log_g.reshape(B, S, GLA_HEADS, GLA_K_DIM)).astype(x.dtype)
        y_gla = rmsnorm(o, gla_norm[i]).reshape(B, S, GLA_WIDTH)
        y = jnp.concatenate([y_att * jax.nn.silu(za), y_gla * jax.nn.silu(zg)], axis=-1) @ w_out[i]
        h = h + y
        gate = jax.nn.sigmoid(rmsnorm(h, ple_norm[i]) @ w_ple_gate[i])
        h = h + gate * (p[i] @ w_ple_proj[i])
    return rmsnorm(h, final_norm)
```

```python
import numpy as np
from contextlib import ExitStack
import concourse.bass as bass
import concourse.mybir as mybir
from concourse.bass_utils import run_bass_kernel_spmd

F32 = mybir.dt.float32
BF16 = mybir.dt.bfloat16
I32 = mybir.dt.int32
AF = mybir.ActivationFunctionType
ALU = mybir.AluOpType

NCORES = 8
TOK = 2048
NT = 16
D = 1024
DIN = 5392
EPS = 1e-6
PI = float(np.float32(np.pi))
TWO_PI = float(np.float32(2 * np.pi))
CW1 = 6.28125
CW2 = float(np.float32(round((2 * np.pi - CW1) * 2 ** 20) / 2 ** 20))
CW3 = float(np.float32(2 * np.pi - CW1 - CW2))

C_AQ, C_AK, C_AV, C_GQ, C_GK, C_GV, C_GL, C_ZA, C_ZG = 0, 1024, 1152, 1280, 1792, 2304, 3328, 3344, 4368


class Sched:
    ENG = ("pe", "act", "dve", "pool", "sp")
    LIST_SCHEDULE = True

    def __init__(self):
        self.items = []
        self.alias = {}
        self.open_pe = None
        self.parity = 0
        self.prog = {e: [] for e in self.ENG}
        self.cnt = {e: 0 for e in self.ENG}
        self.seen = {e: {} for e in self.ENG}
        self.dma_cnt = {}

    def keys(self, names):
        out = []
        for n in names:
            if _DBG.get("noalias") and n in _DBG["noalias"]:
                ks = (n,)
            else:
                ks = self.alias.get(n, (n,))
            if _DBG.get("dbl") and n in _DBG["dbl"]:
                ks = tuple((k, self.parity) for k in ks)
            out.extend(ks)
        return out

    def op(self, eng, fn, reads=(), writes=(), inc=True, dma=None, cost=0.3, dma_inc=16):
        reads = self.keys(reads)
        writes = self.keys(writes)
        if eng == "pe":
            if self.open_pe is None:
                g = dict(eng="pe", fns=[], reads=[], writes=[], dma=None, cost=0.0)
                self.items.append(g)
                self.open_pe = g
            g = self.open_pe
            g["fns"].append(fn)
            g["reads"].extend(reads)
            g["writes"].extend(writes)
            g["cost"] += cost
            if inc:
                self.open_pe = None
            return
        assert self.open_pe is None, "non-PE op recorded inside an open PE group"
        self.items.append(dict(eng=eng, fns=[fn], reads=list(reads), writes=list(writes), dma=dma, cost=cost,
                               dma_inc=dma_inc))

    def barrier(self):
        assert self.open_pe is None
        self.items.append(("barrier",))

    @staticmethod
    def _is_bank(k):
        return isinstance(k, str) and len(k) == 2 and k[0] == "B" and k[1].isdigit()

    def _hazards(self, ops):
        lastw, readers = {}, {}
        for idx, o in enumerate(ops):
            preds = set()
            for r in o["reads"]:
                if r in lastw:
                    preds.add(lastw[r])
                if self._is_bank(r):
                    preds.update(readers.get(r, ()))
            for w in o["writes"]:
                if w in lastw:
                    preds.add(lastw[w])
                preds.update(readers.get(w, ()))
            preds.discard(idx)
            o["preds"] = preds
            for w in o["writes"]:
                lastw[w] = idx
                readers[w] = set()
            for r in o["reads"]:
                readers.setdefault(r, set()).add(idx)

    def _order(self, ops):
        n = len(ops)
        if not self.LIST_SCHEDULE:
            return list(range(n))
        LAT = 0.15
        succs = [[] for _ in range(n)]
        npred = [0] * n
        for i, o in enumerate(ops):
            npred[i] = len(o["preds"])
            for p in o["preds"]:
                succs[p].append(i)
        bl = [0.0] * n
        for i in range(n - 1, -1, -1):
            m = 0.0
            for sidx in succs[i]:
                if bl[sidx] + LAT > m:
                    m = bl[sidx] + LAT
            bl[i] = ops[i]["cost"] + m
        ready_t = [0.0] * n
        finish = [0.0] * n
        free = {e: 0.0 for e in self.ENG}
        ready = {e: [] for e in self.ENG}
        for i in range(n):
            if npred[i] == 0:
                ready[ops[i]["eng"]].append(i)
        picks = []
        done = 0
        mode = _DBG.get("sched", "bl")
        while done < n:
            best = None
            for e in self.ENG:
                lst = ready[e]
                if not lst:
                    continue
                fe = free[e]
                bi, bkey = None, None
                for i in lst:
                    t = ready_t[i] if ready_t[i] > fe else fe
                    if mode == "bl":
                        key = (t - fe if t - fe > 0.05 else 0.0, -bl[i], i)
                    else:
                        key = (t, i)
                    if bkey is None or key < bkey:
                        bi, bkey = i, key
                t = ready_t[bi] if ready_t[bi] > fe else fe
                if best is None or (t, bi) < (best[1], best[0]):
                    best = (bi, t, e)
            i, t, e = best
            ready[e].remove(i)
            o = ops[i]
            if o["dma"]:
                free[e] = t + 0.1
                finish[i] = t + o["cost"]
            else:
                free[e] = t + o["cost"]
                finish[i] = free[e]
            picks.append((t, i))
            done += 1
            for sidx in succs[i]:
                if finish[i] + LAT > ready_t[sidx]:
                    ready_t[sidx] = finish[i] + LAT
                npred[sidx] -= 1
                if npred[sidx] == 0:
                    ready[ops[sidx]["eng"]].append(sidx)
        self.est_time = getattr(self, "est_time", 0.0) + (max(finish) if n else 0.0)
        return [i for (_, i) in picks]

    def _emit_segment(self, ops):
        self._hazards(ops)
        order = self._order(ops)
        know = {}
        pos = {i: n for n, i in enumerate(order)}
        for i in order:
            o = ops[i]
            eng = o["eng"]
            own = None if o["dma"] else "S_" + eng
            sd = self.seen[eng]
            for p in sorted(o["preds"], key=lambda q: -pos[q]):
                sem, val = ops[p]["sv"]
                if sem == own and eng == "pe":
                    continue
                if sd.get(sem, 0) >= val:
                    continue
                self.prog[eng].append(("wait", sem, val))
                for s2, v2 in know[p].items():
                    if sd.get(s2, 0) < v2:
                        sd[s2] = v2
            if o["dma"]:
                di = o.get("dma_inc", 16)
                self.dma_cnt[o["dma"]] = self.dma_cnt.get(o["dma"], 0) + di
                sem, val, incn = o["dma"], self.dma_cnt[o["dma"]], di
            else:
                self.cnt[eng] += 1
                sem, val, incn = own, self.cnt[eng], 1
            o["sv"] = (sem, val)
            kn = dict(sd)
            kn[sem] = val
            know[i] = kn
            fns = o["fns"]
            for k, fn in enumerate(fns):
                self.prog[eng].append(("op", fn, sem, incn if k == len(fns) - 1 else 0))

    def _emit_barrier(self):
        tot = {("S_" + e): self.cnt[e] for e in ("pe", "act", "dve", "pool")}
        tot.update(self.dma_cnt)
        for e in self.ENG:
            for sem, val in tot.items():
                if val > 0 and sem != "S_" + e and self.seen[e].get(sem, 0) < val:
                    self.seen[e][sem] = val
                    self.prog[e].append(("wait", sem, val))

    def finalize(self):
        assert self.open_pe is None
        seg = []
        for it in self.items + [("barrier",)]:
            if isinstance(it, tuple):
                if seg:
                    self._emit_segment(seg)
                    seg = []
                self._emit_barrier()
            else:
                seg.append(it)


def _emit(nc, S, es):
    S.finalize()
    sem_names = ["S_pe", "S_act", "S_dve", "S_pool"] + sorted(S.dma_cnt)
    sems = {n: es.enter_context(nc.semaphore(n)) for n in sem_names}
    block = es.enter_context(nc.Block())

    def run(eng_name):
        def f(e):
            for it in S.prog[eng_name]:
                if it[0] == "wait":
                    e.wait_ge(sems[it[1]], it[2])
                else:
                    _, fn, sem, incn = it
                    ins = fn(e)
                    if incn:
                        ins.then_inc(sems[sem], incn)
        return f

    block.sync(run("sp"))
    block.tensor(run("pe"))
    block.scalar(run("act"))
    block.vector(run("dve"))
    block.gpsimd(run("pool"))
    return nc


_DBG = {"level": 9, "tiles": NT, "p1tiles": NT, "stop": None}


class _Stop(Exception):
    pass


def ck(n):
    if _DBG["stop"] == n:
        raise _Stop()


def build_nc():
    nc = bass.Bass("TRN2", target_bir_lowering=False)
    LVL = _DBG["level"]

    def din(name, shape, dt=F32):
        return nc.dram_tensor(name, list(shape), dt, kind="ExternalInput").ap()

    x_d = din("x", [TOK, D])
    xh_d = din("xh", [128, D])
    p_d = din("p", [TOK, 256])
    pos_d = din("pos", [128, 17], I32)
    win_d = din("w_in", [D, DIN])
    wout_d = din("w_out", [2048, D])
    wpg_d = din("w_pg", [D, D])
    wpp_d = din("w_pp", [256, D])
    nm_d = din("nm", [128, 8])
    pn_d = din("pn", [128, 8])
    gn_d = din("gn", [128, 2])
    fn_d = din("fn", [128, D])
    sinks_d = din("sinks", [128, 16])
    wgu_d = din("wgu", [17, 512])
    ident_d = din("ident", [128, 128])
    tri_d = din("tri", [128, 128])
    tric_d = din("tric", [128, 128])
    mprev0_d = din("mprev0", [128, 128])
    flags_d = din("flags", [128, 3])
    invf_d = din("invf", [128, 8])
    out_d = nc.dram_tensor("out", [TOK, D], F32, kind="ExternalOutput").ap()
    cc_in = nc.dram_tensor("cc_in", [128, 1040], F32)
    cc_out = nc.dram_tensor("cc_out", [512, 1040], F32)
    sc_e1 = nc.dram_tensor("sc_e1", [NT * 128, 512], F32).ap()
    sc_ki = nc.dram_tensor("sc_ki", [NT * 128, 512], BF16).ap()
    sc_ktl = nc.dram_tensor("sc_ktl", [NT * 128, 512], BF16).ap()
    sc_vg = nc.dram_tensor("sc_vg", [NT * 128, 1024], BF16).ap()

    S = Sched()
    es = ExitStack()
    with es:
        def sb(name, shape, dt):
            return es.enter_context(nc.sbuf_tensor(name, list(shape), dt))

        def psb(name, shape, dt):
            return es.enter_context(nc.psum_tensor(name, list(shape), dt))

        WIN = sb("WIN", [128, 8, DIN], BF16)
        WOUT = sb("WOUT", [128, 16, D], BF16)
        WPG = sb("WPG", [128, 8, D], BF16)
        WPP = sb("WPP", [128, 2, D], BF16)
        IDENT = sb("IDENT", [128, 128], F32)
        IDB = sb("IDB", [128, 128], BF16)
        TRI = sb("TRI", [128, 128], F32)
        TRIC = sb("TRIC", [128, 128], F32)
        MASK2 = sb("MASK2", [128, 2, 128], BF16)
        MASK0 = sb("MASK0", [128, 2, 128], BF16)
        FN = sb("FN", [128, D], F32)
        CC = sb("CC", [128, 17, 16], F32)
        SS = sb("SS", [128, 17, 16], F32)
        WGU = sb("WGU", [32, 512], F32)
        GLT = sb("GLT", [32, 128], F32)
        ESINK = sb("ESINK", [128, 16], F32)
        NM = sb("NM", [128, 8], F32)
        PN = sb("PN", [128, 8], F32)
        GN = sb("GN", [128, 2], F32)
        FLAGS = sb("FLAGS", [128, 3], F32)
        ONES = sb("ONES", [128, 2], F32)
        ST = sb("ST", [128, 32], F32)
        DLOG = sb("DLOG", [128, 4], F32)
        DEC = sb("DEC", [128, 4], F32)
        X = [sb("X0", [128, D], F32)]
        XN = sb("XN", [128, D], BF16)
        UT = sb("UT", [128, 8, 128], BF16)
        XNs = [XN, sb("XNb", [128, D], BF16)]
        UTs = [UT, sb("UTb", [128, 8, 128], BF16)]
        cur = {"UT": UT, "kUT": "UT"}
        K2 = sb("K2", [128, 2, 2, 64], BF16)
        KTD = [sb("KTD0", [128, 2, 128], BF16), sb("KTD1", [128, 2, 128], BF16)]
        VA = [sb("VA0", [128, 2, 65], BF16), sb("VA1", [128, 2, 65], BF16)]
        VG = sb("VG", [128, 4, 256], BF16)
        SA = sb("SA", [128, D], BF16)
        SG = sb("SG", [128, D], BF16)
        PT = [sb("PT%d" % i, [128, 2, 128], BF16) for i in range(4)]
        Y = sb("Y", [128, 2048], BF16)
        SST = sb("SST", [128, 4, 256], F32)
        SB = sb("SB", [128, 4, 256], BF16)
        PF = sb("PF", [128, 256], F32)
        GL = sb("GL", [128, 16], F32)
        RT = [sb("RT0", [128, 128], F32), sb("RT1", [128, 128], F32)]
        GL1 = PF[:, 128:144]
        GLT1 = PF[0:32, 0:128]
        DEC1 = PF[:, 144:148]
        DECALL = sb("DECALL", [128, NT, 4], F32)
        R1 = sb("R1", [128, 6400], F32)

        def r1(off, nbytes, dt):
            v = R1[:, off // 4:(off + nbytes) // 4]
            return v if dt == F32 else v.bitcast(dt)

        def r1keys(off, nbytes):
            return tuple(("R1", g) for g in range(off // 1024, (off + nbytes + 1023) // 1024))

        def r1buf(name, off, nbytes, dt):
            S.alias[name] = r1keys(off, nbytes)
            return r1(off, nbytes, dt)

        QT = r1buf("QT", 0, 2048, BF16)
        TS = [r1buf("TS0", 2048, 2048, F32), r1buf("TS1", 4096, 2048, F32)]
        E1 = r1buf("E1", 6144, 2048, F32)
        QD = r1buf("QD", 8192, 1024, BF16)
        QDT = r1buf("QDT", 9216, 1024, BF16)
        KIT = r1buf("KIT", 10240, 1024, BF16)
        KTL = r1buf("KTL", 11264, 1024, BF16)
        OG = r1buf("OG", 16384, 4096, F32)
        YT = r1buf("YT", 12288, 4096, BF16)
        HN = r1buf("HN", 12288, 2048, BF16)
        HNT = r1buf("HNT", 14336, 2048, BF16)
        H = r1buf("H", 16384, 4096, F32)
        SIG = r1buf("SIG", 20480, 4096, F32)
        OUT = r1buf("OUT", 20480, 4096, F32)
        PB = r1buf("PB", 24576, 512, BF16)
        PTT = r1buf("PTT", 25088, 512, BF16)
        S.alias["SIGa"] = r1keys(20480, 2048)
        S.alias["SIGb"] = r1keys(22528, 2048)
        P1LG = r1buf("P1LG", 4096, 2048, F32)
        P1E3 = r1buf("P1E3", 10240, 2048, F32)
        P1KTL = r1buf("P1KTL", 14336, 1024, BF16)
        STG = [r1buf("STG%d" % i, i * 6208, 6208, F32) for i in range(4)]
        GX = r1buf("GX", 0, 3 * 1040 * 4, F32)
        Yf = Y[:].bitcast(F32)
        SAf = SA[:].bitcast(F32)
        ROPE = [Yf[:, i * 136:(i + 1) * 136] for i in range(7)] + [SAf[:, 0:136]]
        ROPEI = SAf[:, 136:272].bitcast(I32)
        POSI = SAf[:, 272:289].bitcast(I32)
        MP0 = SAf[:, 384:512]
        LGb = r1buf("LGb", 0, 2048, F32)
        E3b = r1buf("E3b", 2048, 2048, F32)
        KTLb = r1buf("KTLb", 6144, 1024, BF16)
        E2b = r1buf("E2b", 7168, 2048, F32)
        KIa = r1buf("KIa", 12288, 1024, BF16)
        KIb = r1buf("KIb", 13312, 1024, BF16)
        CCS = r1buf("CCS", 0, 4160, F32)
        S.alias["INVF"] = ("ROPE0",)
        for a_, b_ in (("XN0", "XN"), ("XN1", "XNb"), ("UTp0", "UT"), ("UTp1", "UTb"),
                       ("ss0_0", "p1ss0"), ("rs0_0", "p1rs0"), ("ss0_1", "p1ss1"), ("rs0_1", "p1rs1")):
            S.alias[a_] = (b_,)
        S.alias["YTa"] = r1keys(12288, 2048)
        S.alias["YTb"] = r1keys(14336, 2048)
        for nm_, bk in (("B2a", "B2"), ("B2b", "B2"), ("B3a", "B3"), ("B3b", "B3"), ("B3c", "B3"), ("B3d", "B3"),
                        ("B4L", "B4"), ("B4R", "B4"), ("B5L", "B5"), ("B5R", "B5"), ("B6L", "B6"), ("B6R", "B6")):
            S.alias[nm_] = (bk,)
        if _DBG.get("vtp"):
            S.alias["B2a"] = ("VTa",)
            S.alias["B2b"] = ("VTb",)
        S.alias["ssg"] = ("ssg0", "ssg1", "ssg2", "ssg3")
        for h in range(4):
            S.alias["OG%d" % h] = r1keys(16384 + h * 1024, 1024)
            S.alias["OGx%d" % h] = r1keys(16384 + h * 1024, 1024)
        S.alias["Yall"] = ("Ya0", "Ya1", "Ya2", "Ya3", "Yg0", "Yg1", "Yg2", "Yg3")

        BK = [psb("B%d" % i, [128, 512], F32) if i != 2 else None for i in range(8)]
        B2bf = psb("B2", [128, 1024], BF16)[:]
        B2f32 = B2bf.bitcast(F32)

        def fsz(ap):
            n = 1
            for d in ap.shape[1:]:
                n *= d
            return n

        def ecost(eng, ap):
            n = fsz(ap)
            if eng == "act":
                return n / 1130.0 + 0.17
            if eng == "dve":
                return n / 950.0 + 0.15
            return n / 640.0 + 0.3

        def dma(out, in_, sem, reads=(), writes=(), eng="sp"):
            nb = fsz(out) * out.shape[0] * 4
            return S.op(eng, lambda e, o=out, i=in_: e.dma_start(out=o, in_=i), reads, writes, dma=sem,
                        cost=2.0 + nb / 150e3)

        def dmaT(out, in_, sem, reads=(), writes=()):
            nb = fsz(out) * out.shape[0] * 2
            return S.op("sp", lambda e, o=out, i=in_: e.dma_start_transpose(out=o, in_=i), reads, writes, dma=sem,
                        cost=2.5 + nb / 150e3)

        def mm(out, lhsT, rhs, start, stop, reads, writes, inc=None):
            inc = stop if inc is None else inc
            c = fsz(rhs) * (4 if lhsT.dtype == F32 else 1) / _DBG.get("pe_rate", 1800.0) + _DBG.get("pe_fix", 0.12)
            return S.op("pe", lambda e, o=out, l=lhsT, r=rhs, s=start, t=stop:
                        e.matmul(out=o, lhsT=l, rhs=r, start=s, stop=t), reads, writes, inc=inc, cost=c)

        def tp(out, in_, ident, reads, writes, inc=True):
            return S.op("pe", lambda e, o=out, i=in_, d=ident: e.transpose(out=o, in_=i, identity=d),
                        reads, writes, inc=inc, cost=0.1)

        def act(out, in_, func, reads, writes, scale=None, bias=None, accum=None):
            kw = {}
            if scale is not None:
                kw["scale"] = scale
            if bias is not None:
                kw["bias"] = bias
            if accum is not None:
                kw["accum_out"] = accum
            return S.op("act", lambda e, o=out, i=in_, f=func, k=kw: e.activation(out=o, in_=i, func=f, **k),
                        reads, writes, cost=ecost("act", out))

        def tt(eng, out, in0, in1, op, reads, writes):
            return S.op(eng, lambda e, o=out, a=in0, b=in1, p=op: e.tensor_tensor(out=o, in0=a, in1=b, op=p),
                        reads, writes, cost=ecost(eng, out))

        def tsc(eng, out, in0, s1, op0, reads, writes, s2=None, op1=None):
            if op1 is None:
                return S.op(eng, lambda e, o=out, a=in0, s=s1, p=op0:
                            e.tensor_scalar(out=o, in0=a, scalar1=s, scalar2=None, op0=p), reads, writes,
                            cost=ecost(eng, out))
            return S.op(eng, lambda e, o=out, a=in0, s=s1, p=op0, s_2=s2, p1=op1:
                        e.tensor_scalar(out=o, in0=a, scalar1=s, scalar2=s_2, op0=p, op1=p1), reads, writes,
                        cost=ecost(eng, out))

        def stt(eng, out, in0, scalar, in1, op0, op1, reads, writes):
            return S.op(eng, lambda e, o=out, a=in0, s=scalar, b=in1, p0=op0, p1=op1:
                        e.scalar_tensor_tensor(out=o, in0=a, scalar=s, in1=b, op0=p0, op1=p1), reads, writes,
                        cost=ecost(eng, out))

        def cp(eng, out, in_, reads, writes):
            if eng == "act":
                return act(out, in_, AF.Copy, reads, writes)
            return S.op(eng, lambda e, o=out, i=in_: e.tensor_copy(out=o, in_=i), reads, writes,
                        cost=ecost(eng, out))

        def memset(eng, ap, val, writes):
            return S.op(eng, lambda e, a=ap, v=val: e.memset(a, v), (), writes, cost=ecost(eng, ap))

        def rstd_from_ss(ss_ap, out_ap, n, key_ss, key_out):
            act(out_ap, ss_ap, AF.Ln, [key_ss], [key_out], scale=1.0 / n, bias=EPS)
            act(out_ap, out_ap, AF.Exp, [key_out], [key_out], scale=-0.5)

        dma(IDENT[:], ident_d, "d_c0", writes=["IDENT"])
        dma(TRI[:], tri_d, "d_c1", writes=["TRI"])
        dma(TRIC[:], tric_d, "d_c2", writes=["TRIC"])
        dma(MP0, mprev0_d, "d_c3", writes=["MP0"])
        dma(ESINK[:], sinks_d, "d_c5", writes=["ESINK"])
        dma(NM[:], nm_d, "d_c6", writes=["NM"])
        dma(PN[:], pn_d, "d_c7", writes=["PN"])
        dma(GN[:], gn_d, "d_c8", writes=["GN"])
        dma(FLAGS[:], flags_d, "d_c9", writes=["FLAGS"])
        dma(WGU[0:17, :], wgu_d, "d_c10", writes=["WGU"])
        dma(POSI, pos_d, "d_c11", writes=["POSI"])
        dma(ROPE[0][:, 0:8], invf_d, "d_c12", writes=["INVF"])
        if LVL < 0.15:
            S.barrier()
            return _emit(nc, S, es)
        cp("dve", IDB[:], IDENT[:], ["IDENT"], ["IDB"])
        cp("dve", MASK2[:, 0, :], TRIC[:], ["TRIC"], ["MASK2"])
        cp("dve", MASK2[:, 1, :], TRI[:], ["TRI"], ["MASK2"])
        cp("dve", MASK0[:, 0, :], MP0, ["MP0"], ["MASK0"])
        cp("dve", MASK0[:, 1, :], TRI[:], ["TRI"], ["MASK0"])
        memset("dve", ONES[:], 1.0, ["ONES"])
        memset("dve", GLT[:], 1.0, ["GLT"])
        memset("dve", GLT1, 1.0, ["GLT1"])
        memset("dve", DLOG[:], 0.0, ["DLOG"])
        memset("dve", SST[:], 0.0, ["SST"])
        memset("dve", SB[:], 0.0, ["SB"])
        for b in range(2):
            memset("dve", VA[b][:], 1.0, ["VA%d" % b])
            memset("dve", KTD[b][:], 0.0, ["KTD%d" % b])
        act(ESINK[:], ESINK[:], AF.Exp, ["ESINK"], ["ESINK"])

        if LVL < 0.25:
            S.barrier()
            return _emit(nc, S, es)
        INVF = ROPE[0][:, 0:8]
        POSF = ROPE[1][:, 0:17]
        ANG = ROPE[2][:, 0:136].rearrange("p (t f) -> p t f", f=8)
        KF = ROPE[3][:, 0:136].rearrange("p (t f) -> p t f", f=8)
        RR = ROPE[4][:, 0:136].rearrange("p (t f) -> p t f", f=8)
        R2 = ROPE[5][:, 0:136].rearrange("p (t f) -> p t f", f=8)
        MM_ = ROPE[6][:, 0:136].rearrange("p (t f) -> p t f", f=8)
        SINV = ROPE[7][:, 0:136].rearrange("p (t f) -> p t f", f=8)
        KI32 = ROPEI[:, 0:136].rearrange("p (t f) -> p t f", f=8)
        cp("dve", POSF, POSI, ["POSI"], ["ROPE1"])
        tt("dve", ANG, POSF.unsqueeze(2).broadcast_to([128, 17, 8]),
           INVF.unsqueeze(1).broadcast_to([128, 17, 8]), ALU.mult, ["ROPE1", "ROPE0"], ["ROPE2"])
        tsc("dve", KI32, ANG, 1.0 / (2 * np.pi), ALU.mult, ["ROPE2"], ["ROPEI"])
        cp("dve", KF, KI32, ["ROPEI"], ["ROPE3"])
        stt("dve", RR, KF, -CW1, ANG, ALU.mult, ALU.add, ["ROPE3", "ROPE2"], ["ROPE4"])
        stt("dve", RR, KF, -CW2, RR, ALU.mult, ALU.add, ["ROPE3", "ROPE4"], ["ROPE4"])
        stt("dve", RR, KF, -CW3, RR, ALU.mult, ALU.add, ["ROPE3", "ROPE4"], ["ROPE4"])
        tsc("dve", RR, RR, PI, ALU.min, ["ROPE4"], ["ROPE4"], s2=-PI, op1=ALU.max)
        act(SINV, RR, AF.Sin, ["ROPE4"], ["ROPE7"])
        tsc("dve", R2, RR, float(np.pi / 2), ALU.add, ["ROPE4"], ["ROPE5"])
        tsc("dve", MM_, R2, PI, ALU.is_gt, ["ROPE5"], ["ROPE6"], s2=TWO_PI, op1=ALU.mult)
        tt("dve", R2, R2, MM_, ALU.subtract, ["ROPE5", "ROPE6"], ["ROPE5"])
        tsc("dve", R2, R2, PI, ALU.min, ["ROPE5"], ["ROPE5"], s2=-PI, op1=ALU.max)
        act(CC[:, :, 0:8], R2, AF.Sin, ["ROPE5"], ["CC"])
        cp("dve", CC[:, :, 8:16], CC[:, :, 0:8], ["CC"], ["CC"])
        cp("dve", SS[:, :, 8:16], SINV, ["ROPE7"], ["SS"])
        tsc("dve", SS[:, :, 0:8], SINV, -1.0, ALU.mult, ["ROPE7"], ["SS"])

        STG2 = [r1buf("STGa", 15360, 5120, F32), r1buf("STGb", 20480, 5120, F32)]
        wstate = {"i": 0}

        def wpiece(src, n, dst, sc, slots, tag, engs):
            i = wstate["i"]
            wstate["i"] += 1
            sl = i % len(slots)
            dma(slots[sl][:, 0:n], src, "d_%s%d" % (tag, sl), writes=["%s%d" % (tag, sl)])
            eng = engs[i % len(engs)]
            rd = ["%s%d" % (tag, sl), "NM", "PN", "GN"]
            wk = [("Wp", i)]
            if sc is None:
                cp(eng, dst, slots[sl][:, 0:n], rd, wk)
            elif eng == "act":
                act(dst, slots[sl][:, 0:n], AF.Copy, rd, wk, scale=sc)
            else:
                tsc(eng, dst, slots[sl][:, 0:n], sc, ALU.mult, rd, wk)

        S.alias["STGa0"] = S.alias["STGa"]
        S.alias["STGa1"] = S.alias["STGb"]
        if LVL >= 2:
            for c in range(8):
                wpiece(win_d[c * 128:(c + 1) * 128, C_GK:C_ZA], C_ZA - C_GK, WIN[:, c, C_GK:C_ZA], NM[:, c:c + 1],
                       STG, "STG", ["dve", "act"])
        pieces2 = []
        for c in range(8):
            for (c0, c1) in ((0, 1280), (1280, C_GK), (C_ZA, C_ZA + 1280), (C_ZA + 1280, DIN)):
                pieces2.append((win_d[c * 128:(c + 1) * 128, c0:c1], c1 - c0, WIN[:, c, c0:c1], NM[:, c:c + 1]))
        for c in range(16):
            sc = None if c < 8 else GN[:, (c % 2):(c % 2) + 1]
            pieces2.append((wout_d[c * 128:(c + 1) * 128, :], 1024, WOUT[:, c, :], sc))
        for c in range(8):
            pieces2.append((wpg_d[c * 128:(c + 1) * 128, :], 1024, WPG[:, c, :], PN[:, c:c + 1]))
        for c in range(2):
            pieces2.append((wpp_d[c * 128:(c + 1) * 128, :], 1024, WPP[:, c, :], None))
        if LVL < 2:
            pieces2 = []

        def stream_weights(k):
            for _ in range(k):
                if pieces2:
                    src, n, dst, sc = pieces2.pop(0)
                    wpiece(src, n, dst, sc, STG2, "STGa", ["act", "act", "dve"])
        S.barrier()

        def load_x(xsrc, xb=0):
            dma(X[0][:], xsrc, "d_x0", writes=["X0"])

        def stage_A(xsrc, par, xbuf=None, kx="X0"):
            xb_ = X[0][:] if xbuf is None else xbuf
            XNp, UTp = XNs[par], UTs[par]
            kXN, kUT = "XN%d" % par, "UTp%d" % par
            so = 0 if par == 0 else 6
            kss, krs = "ss0_%d" % par, "rs0_%d" % par
            act(XNp[:], xb_, AF.Square, [kx], [kXN, kss], accum=ST[:, so:so + 1])
            rstd_from_ss(ST[:, so:so + 1], ST[:, so + 1:so + 2], D, kss, krs)
            tsc("dve", XNp[:], xb_, ST[:, so + 1:so + 2], ALU.mult, [kx, krs], [kXN])
            dmaT(UTp[:], XNp[:].rearrange("p (c t) -> p c t", c=8), "d_tu%d" % par, [kXN], [kUT])

        def use_ut(par):
            cur["UT"], cur["kUT"] = UTs[par], "UTp%d" % par

        pbank = [0]

        def proj(col0, ncols, lhs, lhs_key, W, nk, wcol0=None):
            b = (0, 1, 3, 2)[pbank[0] % 4]
            pbank[0] += 1
            key = "B%d" % b
            if _DBG.get("vproj"):
                pbank.append(0)
                key = "VP%d" % (len(pbank) % _DBG["vproj"])
            ps = (BK[b][:] if b != 2 else B2f32)[:, 0:ncols]
            for k in range(nk):
                if isinstance(lhs_key, list):
                    half = nk // len(lhs_key)
                    lk = lhs_key[k // half]
                    ginc = (k % half == half - 1)
                else:
                    lk, ginc = lhs_key, (k == nk - 1)
                mm(ps, lhs[:, k, :], W[:, k, col0:col0 + ncols], k == 0, k == nk - 1,
                   [lk, "W"], [key], inc=ginc)
            return ps, key

        def gla_kv_state(h, t):
            hk = "B4%s" % "LR"[h % 2]
            ps = BK[4][:, (h % 2) * 256:(h % 2) * 256 + 256]
            mm(ps, KTL[:, h * 128:(h + 1) * 128], VG[:, h, :], True, True, ["KTL", "VG"], [hk])
            tsc("dve", SST[:, h, :], SST[:, h, :], DECALL[:, t, h:h + 1], ALU.mult, ["SST%d" % h], ["SST%d" % h])
            stt("dve", SST[:, h, :], ps, DECALL[:, t, h:h + 1], SST[:, h, :], ALU.mult, ALU.add,
                [hk, "SST%d" % h], ["SST%d" % h])

        p1bank = [0]
        P1_BANKS = [0, 1, 6, 7, 2]

        def p1_proj(col0, ncols, lhs, lhs_key):
            b = P1_BANKS[p1bank[0] % len(P1_BANKS)]
            p1bank[0] += 1
            key = "B%d" % b
            ps = (BK[b][:] if b != 2 else B2f32)[:, 0:ncols]
            for k in range(8):
                mm(ps, lhs[:, k, :], WIN[:, k, col0:col0 + ncols], k == 0, k == 7, [lhs_key, "W"], [key])
            return ps, key

        def p1_tile(t):
            par = (t + 1) % 2
            sfx = "" if par == 0 else "b"
            XNp = XNs[par][:]
            UTp = UTs[par][:]
            VGp = VG[:] if par == 0 else Y[:, 0:1024].rearrange("p (h d) -> p h d", h=4)
            E1p = (SA if par == 0 else SG)[:].bitcast(F32)
            E2p = Y[:, 1024:2048].bitcast(F32) if par == 0 else E2b
            KIp = KIa if par == 0 else KIb
            kE1, kE2, kKI = ("SA", "Yp1b", "KIa") if par == 0 else ("SG", "E2b", "KIb")
            LGp, E3p, KTLp = (P1LG, P1E3, P1KTL) if par == 0 else (LGb, E3b, KTLb)
            GLp, GLTp, DECp = (GL, GLT, DEC) if par == 0 else (GL1, GLT1, DEC1)
            kXN, kUT, kVG = ("XN", "UT", "VG") if par == 0 else ("XNb", "UTb", "Yp1")
            kLG, kE3, kKTL = ("P1LG", "P1E3", "P1KTL") if par == 0 else ("LGb", "E3b", "KTLb")
            kGL, kGLT, kDEC = "GL%d" % par, "GLT%d" % par, "DEC%d" % par
            so = 0 if par == 0 else 6
            kss, krs = "p1ss%d" % par, "p1rs%d" % par
            kx = "X0"
            act(XNp, X[0][:], AF.Square, [kx], [kXN, kss], accum=ST[:, so:so + 1])
            rstd_from_ss(ST[:, so:so + 1], ST[:, so + 1:so + 2], D, kss, krs)
            tsc("dve", XNp, X[0][:], ST[:, so + 1:so + 2], ALU.mult, [kx, krs], [kXN])
            if t + 1 < _DBG["p1tiles"]:
                load_x(x_d[(t + 1) * 128:(t + 2) * 128, :])
            dmaT(UTp, XNp.rearrange("p (c t) -> p c t", c=8), "d_tu%d" % par, [kXN], [kUT])
            for k in range(8):
                mm(BK[3][0:16, 0:128], WIN[:, k, C_GL:C_GL + 16], UTp[:, k, :], k == 0, k == 7, [kUT, "W"], ["B3"])
            cp("dve", GLTp[0:16, :], BK[3][0:16, 0:128], ["B3"], [kGLT])
            mm(BK[4][:], GLTp[0:17, :], WGU[0:17, :], True, True, [kGLT, "WGU"], ["B4"])
            act(LGp, BK[4][:], AF.Exp, ["B4"], [kLG], scale=-1.0)
            act(LGp, LGp, AF.Ln, [kLG], [kLG], bias=1.0)
            LHI = E3p.bitcast(BF16)[:, 0:512]
            LLO = E3p.bitcast(BF16)[:, 512:1024]
            cp("act", LHI, LGp, [kLG], [kE3 + "h"])
            tt("dve", LLO, LGp, LHI, ALU.subtract, [kLG, kE3 + "h"], [kE3 + "l"])
            for h in range(4):
                mm(BK[3][:, 128 + h:129 + h], LGp[:, h * 128:(h + 1) * 128], ONES[:, 0:1], True, True,
                   [kLG, "ONES"], ["B3"], inc=(h == 3))
            mm(BK[4][:], MASK2[:, 1, :], LHI, True, False, ["MASK2", kE3 + "h"], ["B4"], inc=False)
            mm(BK[4][:], MASK2[:, 1, :], LLO, False, True, ["MASK2", kE3 + "l"], ["B4"], inc=True)
            tt("dve", DLOG[:], DLOG[:], BK[3][:, 128:132], ALU.add, ["DLOG", "B3"], ["DLOG"])
            act(DECp[:], BK[3][:, 128:132], AF.Exp, ["B3"], [kDEC], scale=-1.0 / 16)
            cp("pool", DECALL[:, t, :], DECp[:], [kDEC], [("DECALL", t)])
            act(E1p, BK[4][:], AF.Exp, ["B4"], [kE1], scale=-1.0 / 16)
            act(E2p, BK[4][:], AF.Exp, ["B4"], [kE2], scale=1.0 / 16)
            rows = slice(t * 128, (t + 1) * 128)
            dma(sc_e1[rows, :], E1p, "d_s1%d" % par, reads=[kE1], writes=[("sc_e1", t)])
            ps, key = p1_proj(C_GK, 512, UTp, kUT)
            tt("dve", KIp, ps, E2p, ALU.mult, [key, kE2], [kKI])
            dma(sc_ki[rows, :], KIp, "d_s3%d" % par, reads=[kKI], writes=[("sc_ki", t)])
            for hf in range(2):
                ps, key = p1_proj(C_GV + hf * 512, 512, UTp, kUT)
                cp("act", VGp[:, hf * 2:hf * 2 + 2, :], ps.rearrange("p (h d) -> p h d", h=2), [key], [kVG])
            dma(sc_vg[rows, :], VGp.rearrange("p h d -> p (h d)"), "d_s4%d" % par, reads=[kVG], writes=[("sc_vg", t)])
            for h in range(4):
                ps = BK[5][:, (h % 2) * 256:(h % 2) * 256 + 256]
                mm(ps, KIp[:, h * 128:(h + 1) * 128], VGp[:, h, :], True, True, [kKI, kVG], ["B5"])
                tsc("dve", SST[:, h, :], SST[:, h, :], DECp[:, h:h + 1], ALU.mult, ["SST%d" % h, kDEC], ["SST%d" % h])
                stt("dve", SST[:, h, :], ps, DECp[:, h:h + 1], SST[:, h, :], ALU.mult, ALU.add,
                    ["B5", kDEC, "SST%d" % h], ["SST%d" % h])

        def _phases():
            P1T = _DBG["p1tiles"] if LVL >= 3 else 0
            if P1T:
                load_x(x_d[0:128, :], 0)
            for t in range(P1T):
                stream_weights(4)
                p1_tile(t)
            stream_weights(1000)
            P2T = _DBG["tiles"]
            if LVL >= 5:
                load_x(xh_d)
                dma(FN[:], x_d[0:128, :], "d_xf", writes=["FN"])
                stage_A(xh_d, 1)
                if P2T > 1:
                    load_x(x_d[128:256, :])
                stage_A(x_d[0:128, :], 0, xbuf=FN[:], kx="FN")
            if LVL >= 4:
                act(DLOG[:], DLOG[:], AF.Exp, ["DLOG"], ["DLOG"], scale=-1.0 / 16)
                S.barrier()
                dma(FN[:], fn_d, "d_c4", writes=["FN"])
                CCSx = r1buf("CCSx", 12288, 4160, F32)
                GXx = r1buf("GXx", 12288, 3 * 1040 * 4, F32)
                CCS3 = CCSx.rearrange("p (h c) -> p h c", h=4)
                memset("dve", CCSx, 0.0, ["CCSx"])
                cp("dve", CCS3[:, :, 0:256], SST[:], ["SST0", "SST1", "SST2", "SST3", "CCSx"], ["CCSx"])
                cp("dve", CCS3[:, :, 256], DLOG[:], ["DLOG", "CCSx"], ["CCSx"])
                dma(cc_in.ap(), CCSx, "d_cc", reads=["CCSx"], writes=["cc_in"])
                S.op("pool", lambda e: e.collective_compute(
                    "AllGather", ALU.bypass, replica_groups=[[0, 1, 2, 3], [4, 5, 6, 7]],
                    ins=[cc_in.ap().opt()], outs=[cc_out.ap().opt()]), ["cc_in"], ["cc_out"], cost=45.0,
                    dma="d_coll", dma_inc=1)
                GXv = GXx.rearrange("p (r h c) -> p r h c", r=3, h=4)
                dma(GXx.rearrange("p (r c) -> p r c", r=3), cc_out.ap()[0:384, :].rearrange("(r p) c -> p r c", p=128),
                    "d_cc2", reads=["cc_out"], writes=["GXx"])
                AR = ST[:, 8:20].rearrange("p (r h) -> p r h", r=3)
                for r in range(3):
                    tsc("dve", AR[:, r, :], GXv[:, r, :, 256], FLAGS[:, r:r + 1], ALU.mult, ["GXx", "FLAGS"], ["AR"],
                        s2=FLAGS[:, r:r + 1], op1=ALU.subtract)
                    tsc("dve", AR[:, r, :], AR[:, r, :], 1.0, ALU.add, ["AR"], ["AR"])
                for r in (1, 2):
                    tsc("dve", GXv[:, r, :, 0:256], GXv[:, r, :, 0:256], FLAGS[:, r:r + 1], ALU.mult,
                        ["GXx", "FLAGS"], ["GXx"])
                for h in range(4):
                    tsc("dve", SST[:, h, :], GXv[:, 0, h, 0:256], FLAGS[:, 0:1], ALU.mult, ["GXx", "FLAGS"], ["SST%d" % h])
                    for r in (1, 2):
                        stt("dve", SST[:, h, :], SST[:, h, :], AR[:, r, h:h + 1], GXv[:, r, h, 0:256],
                            ALU.mult, ALU.add, ["SST%d" % h, "AR", "GXx"], ["SST%d" % h])
                    cp("act", SB[:, h, :], SST[:, h, :], ["SST%d" % h], ["SB%d" % h])

            def rope_evac(ps, key, nh, dst3, dst_key, tt_idx, dup):
                src = ps.rearrange("p (h d) -> p h d", h=nh)
                cc = CC[:, tt_idx, :].unsqueeze(1).broadcast_to([128, nh, 16])
                ssl = SS[:, tt_idx, 0:8].unsqueeze(1).broadcast_to([128, nh, 8])
                ssh = SS[:, tt_idx, 8:16].unsqueeze(1).broadcast_to([128, nh, 8])
                TA = RT[0][:, 0:nh * 16].rearrange("p (h d) -> p h d", h=nh)
                TB = RT[1][:, 0:nh * 16].rearrange("p (h d) -> p h d", h=nh)
                tt("dve", TA, src[:, :, 0:16], cc, ALU.mult, [key, "CC"], ["RT0"])
                tt("dve", TB[:, :, 0:8], src[:, :, 8:16], ssl, ALU.mult, [key, "SS"], ["RT1"])
                tt("dve", TB[:, :, 8:16], src[:, :, 0:8], ssh, ALU.mult, [key, "SS"], ["RT1"])
                if dup:
                    for cpy in range(2):
                        tt("dve", dst3[:, :, cpy, 0:16], TA, TB, ALU.add, ["RT0", "RT1"], [dst_key])
                        cp("act", dst3[:, :, cpy, 16:64], src[:, :, 16:64], [key], [dst_key])
                else:
                    tt("dve", dst3[:, :, 0:16], TA, TB, ALU.add, ["RT0", "RT1"], [dst_key])
                    cp("act", dst3[:, :, 16:64], src[:, :, 16:64], [key], [dst_key])

            def kv_attn_proj(tt_idx, kb):
                ps, key = proj(C_AK, 256, cur["UT"], cur["kUT"], WIN, 8)
                rope_evac(ps[:, 0:128], key, 2, K2[:], "K2", tt_idx, True)
                cp("act", VA[kb][:, :, 0:64], ps[:, 128:256].rearrange("p (g d) -> p g d", g=2), [key], ["VA%d" % kb])
                dmaT(KTD[kb][:], K2[:].rearrange("p g c d -> p g (c d)"), "d_tk%d" % kb, ["K2"], ["KTD%d" % kb])

            def silu_evac(col0, dst, dst_key):
                for hf in range(2):
                    ps, key = proj(col0 + hf * 512, 512, cur["UT"], cur["kUT"], WIN, 8)
                    tk = "TS%d" % hf
                    act(TS[hf], ps, AF.Exp, [key], [tk], scale=-1.0)
                    act(TS[hf], TS[hf], AF.Ln, [tk], [tk], bias=1.0)
                    act(TS[hf], TS[hf], AF.Exp, [tk], [tk], scale=-1.0)
                    tt("dve", dst[:, hf * 512:(hf + 1) * 512], ps, TS[hf], ALU.mult, [key, tk], [dst_key])

            def attention(t, kb):
                pb = kb ^ 1
                mask = MASK0 if t == 0 else MASK2
                mkey = "MASK0" if t == 0 else "MASK2"
                for sg in range(4):
                    g = sg // 2
                    obk = 7
                    okey = "B%d" % obk
                    for i in range(4):
                        h = sg * 4 + i
                        c, r0 = h // 2, (h % 2) * 64
                        sbk = (6, 4)[h % 2]
                        sk = "B%d" % sbk
                        sps = BK[sbk][:, 0:256]
                        pi = h % 4
                        mm(sps[:, 0:128], KTD[pb][r0:r0 + 64, g, :], QT[r0:r0 + 64, c * 128:(c + 1) * 128], True, True,
                           ["KTD%d" % pb, "QT"], [sk], inc=False)
                        mm(sps[:, 128:256], KTD[kb][r0:r0 + 64, g, :], QT[r0:r0 + 64, c * 128:(c + 1) * 128], True, True,
                           ["KTD%d" % kb, "QT"], [sk], inc=True)
                        pk = "PT%d" % pi
                        act(PT[pi][:].rearrange("p a q -> p (a q)"), sps, AF.Exp, [sk], [pk], scale=0.125)
                        tt("dve", PT[pi][:], PT[pi][:], mask[:], ALU.mult, [pk, mkey], [pk])
                        ops = BK[obk][:, i * 65:(i + 1) * 65]
                        mm(ops, PT[pi][:, 0, :], VA[pb][:, g, :], True, False, [pk, "VA%d" % pb], [okey], inc=False)
                        mm(ops, PT[pi][:, 1, :], VA[kb][:, g, :], False, True, [pk, "VA%d" % kb], [okey], inc=True)
                    o3 = BK[obk][:, 0:260].rearrange("p (h d) -> p h d", h=4)
                    DEN = ST[:, 20:24]
                    tt("dve", DEN, o3[:, :, 64], ESINK[:, sg * 4:(sg + 1) * 4], ALU.add, [okey, "ESINK"], ["DEN"])
                    S.op("dve", lambda e, o=DEN: e.reciprocal(out=o, in_=o), ["DEN"], ["DEN"], cost=0.2)
                    ysl = Y[:, sg * 256:(sg + 1) * 256]
                    tt("dve", ysl.rearrange("p (h d) -> p h d", h=4), o3[:, :, 0:64],
                       DEN.unsqueeze(2).broadcast_to([128, 4, 64]), ALU.mult, [okey, "DEN"], ["Ya%d" % sg])
                    tt("pool", ysl, ysl, SA[:, sg * 256:(sg + 1) * 256], ALU.mult, ["Ya%d" % sg, "SA"], ["Ya%d" % sg])

            def gla(t):
                for h in range(4):
                    abk = 6
                    ak = "B%d" % abk
                    aps = BK[abk][:, 256:384]
                    mm(aps, KIT[:, h * 128:(h + 1) * 128], QDT[:, h * 128:(h + 1) * 128], True, True, ["KIT", "QDT"], [ak])
                    AT = PT[h % 2][:, 0, :]
                    tt("dve", AT, aps, TRI[:], ALU.mult, [ak, "TRI"], ["PT%d" % (h % 2)])
                    obk = 7 if h % 2 == 0 else 5
                    ok = "B%d" % obk
                    ops = BK[obk][:, 0:256]
                    mm(ops, AT, VG[:, h, :], True, False, ["PT%d" % (h % 2), "VG"], [ok], inc=False)
                    mm(ops, QDT[:, h * 128:(h + 1) * 128], SB[:, h, :], False, True, ["QDT", "SB%d" % h], [ok], inc=True)
                    ysl = Y[:, 1024 + h * 256:1024 + (h + 1) * 256]
                    act(ysl, ops, AF.Square, [ok], ["Yg%d" % h, "ssg%d" % h], accum=ST[:, 24 + h:25 + h])
                    rstd_from_ss(ST[:, 24 + h:25 + h], ST[:, 28 + h:29 + h], 256, "ssg%d" % h, "rsg%d" % h)
                    stt("dve", ysl, ops, ST[:, 28 + h:29 + h], SG[:, h * 256:(h + 1) * 256], ALU.mult, ALU.mult,
                        [ok, "rsg%d" % h, "SG"], ["Yg%d" % h])
                    gla_kv_state(h, t)
                    cp("pool", SB[:, h, :], SST[:, h, :], ["SST%d" % h], ["SB%d" % h])

            def stage_F(t, xb):
                dma(PF[:], p_d[t * 128:(t + 1) * 128, :], "d_p", writes=["PF"])
                YT3w = YT.rearrange("p (c t) -> p c t", c=16)
                Y3 = Y[:].rearrange("p (c t) -> p c t", c=16)
                dmaT(YT3w[:, 0:8, :], Y3[:, 0:8, :], "d_ty0", ["Ya0", "Ya1", "Ya2", "Ya3"], ["YTa"])
                dmaT(YT3w[:, 8:16, :], Y3[:, 8:16, :], "d_ty1", ["Yg0", "Yg1", "Yg2", "Yg3"], ["YTb"])
                YT3 = YT.rearrange("p (c t) -> p c t", c=16)
                dma(H, x_d[t * 128:(t + 1) * 128, :], "d_xr", writes=["H"])
                for hf in range(2):
                    ps, key = proj(hf * 512, 512, YT3, ["YTa", "YTb"], WOUT, 16)
                    tt("dve", H[:, hf * 512:(hf + 1) * 512], ps, H[:, hf * 512:(hf + 1) * 512], ALU.add,
                       [key, "H"], ["H"])
                act(HN, H, AF.Square, ["H"], ["HN", "ss1"], accum=ST[:, 2:3])
                rstd_from_ss(ST[:, 2:3], ST[:, 3:4], D, "ss1", "rs1")
                tsc("dve", HN, H, ST[:, 3:4], ALU.mult, ["H", "rs1"], ["HN"])
                dmaT(HNT.rearrange("p (c t) -> p c t", c=8), HN.rearrange("p (c t) -> p c t", c=8), "d_th", ["HN"], ["HNT"])
                HNT3 = HNT.rearrange("p (c t) -> p c t", c=8)
                for hf in range(2):
                    ps, key = proj(hf * 512, 512, HNT3, "HNT", WPG, 8)
                    tk = "SIG%s" % "ab"[hf]
                    sg = SIG[:, hf * 512:(hf + 1) * 512]
                    act(sg, ps, AF.Exp, [key], [tk], scale=-1.0)
                    act(sg, sg, AF.Ln, [tk], [tk], bias=1.0)
                    act(sg, sg, AF.Exp, [tk], [tk], scale=-1.0)
                cp("pool", PB, PF[:], ["PF"], ["PB"])
                dmaT(PTT.rearrange("p (c t) -> p c t", c=2), PB.rearrange("p (c t) -> p c t", c=2), "d_tp", ["PB"], ["PTT"])
                PTT3 = PTT.rearrange("p (c t) -> p c t", c=2)
                for hf in range(2):
                    ps, key = proj(hf * 512, 512, PTT3, "PTT", WPP, 2)
                    tk = "SIG%s" % "ab"[hf]
                    tt("dve", SIG[:, hf * 512:(hf + 1) * 512], SIG[:, hf * 512:(hf + 1) * 512], ps, ALU.mult,
                       [tk, key], [tk])
                tt("dve", H, H, SIG, ALU.add, ["H", "SIG"], ["H"])
                act(OUT, H, AF.Square, ["H"], ["OUT", "ss2"], accum=ST[:, 4:5])
                rstd_from_ss(ST[:, 4:5], ST[:, 5:6], D, "ss2", "rs2")
                stt("dve", OUT, H, ST[:, 5:6], FN[:], ALU.mult, ALU.mult, ["H", "rs2", "FN"], ["OUT"])
                dma(out_d[t * 128:(t + 1) * 128, :], OUT, "d_o", reads=["OUT"], writes=["out"])

            if LVL >= 5:
                use_ut(1)
                kv_attn_proj(0, 1)

                def front(t):
                    use_ut(t % 2)
                    kb = t % 2
                    kv_attn_proj(t + 1, kb)
                    Qb, kQ = XNs[t % 2], "XN%d" % (t % 2)
                    for hf in range(2):
                        ps, key = proj(C_AQ + hf * 512, 512, cur["UT"], cur["kUT"], WIN, 8)
                        rope_evac(ps, key, 8, Qb[:, hf * 512:(hf + 1) * 512].rearrange("p (h d) -> p h d", h=8), kQ,
                                  t + 1, False)
                    dmaT(QT.rearrange("p (c t) -> p c t", c=8), Qb[:].rearrange("p (c t) -> p c t", c=8), "d_tq", [kQ], ["QT"])
                    silu_evac(C_ZA, SA[:], "SA")
                    if t + 1 < P2T:
                        stage_A(x_d[(t + 1) * 128:(t + 2) * 128, :], (t + 1) % 2)
                        if t + 2 < P2T:
                            load_x(x_d[(t + 2) * 128:(t + 3) * 128, :])

                def back(t):
                    use_ut(t % 2)
                    kb = t % 2
                    rows = slice(t * 128, (t + 1) * 128)
                    dma(E1, sc_e1[rows, :], "d_l1", writes=["E1"])
                    dmaT(KIT.rearrange("p (c t) -> p c t", c=4), sc_ki[rows, :].rearrange("p (c t) -> p c t", c=4),
                         "d_l2", [], ["KIT"])
                    dma(KTL, sc_ki[rows, :], "d_l3", writes=["KTL"])
                    dma(VG[:].rearrange("p h d -> p (h d)"), sc_vg[rows, :], "d_l4", writes=["VG"])
                    attention(t, kb)
                    silu_evac(C_ZG, SG[:], "SG")
                    ps, key = proj(C_GQ, 512, cur["UT"], cur["kUT"], WIN, 8)
                    stt("dve", QD, ps, float(128 ** -0.5), E1, ALU.mult, ALU.mult, [key, "E1"], ["QD"])
                    dmaT(QDT.rearrange("p (c t) -> p c t", c=4), QD.rearrange("p (c t) -> p c t", c=4), "d_tqd", ["QD"], ["QDT"])
                    gla(t)

                front(0)
                for t in range(P2T):
                    S.parity = t % 2
                    back(t)
                    if t + 1 < P2T:
                        front(t + 1)
                    stage_F(t, 0)

        try:
            _phases()
        except _Stop:
            pass

        S.barrier()

        return _emit(nc, S, es)


_NC_CACHE = {}


def _get_nc():
    if "nc" not in _NC_CACHE:
        _NC_CACHE["nc"] = build_nc()
    return _NC_CACHE["nc"]


def kernel(x, p, positions, norm_mix, w_in, attn_sinks, w_gate_up, b_gate, gla_norm,
           w_out, ple_norm, w_ple_gate, w_ple_proj, final_norm):
    f32 = np.float32
    x = np.asarray(x, f32)
    p = np.asarray(p, f32)
    positions = np.asarray(positions, np.int32)
    w_in0 = np.ascontiguousarray(np.asarray(w_in, f32)[0])
    w_out0 = np.ascontiguousarray(np.asarray(w_out, f32)[0])
    w_pg0 = np.ascontiguousarray(np.asarray(w_ple_gate, f32)[0])
    w_pp0 = np.ascontiguousarray(np.asarray(w_ple_proj, f32)[0])
    nm = np.ascontiguousarray(np.asarray(norm_mix, f32)[0].reshape(8, 128).T)
    pn = np.ascontiguousarray(np.asarray(ple_norm, f32)[0].reshape(8, 128).T)
    gn = np.ascontiguousarray(np.asarray(gla_norm, f32)[0].reshape(2, 128).T)
    fn = np.ascontiguousarray(np.broadcast_to(np.asarray(final_norm, f32)[None, :], (128, D)))
    sinks = np.ascontiguousarray(np.broadcast_to(np.asarray(attn_sinks, f32)[0][None, :], (128, 16)))
    wgu = np.ascontiguousarray(np.concatenate([np.asarray(w_gate_up, f32)[0], np.asarray(b_gate, f32)[0][None, :]], 0))
    ident = np.eye(128, dtype=f32)
    ii = np.arange(128)
    tri = (ii[:, None] <= ii[None, :]).astype(f32)
    tric = np.ascontiguousarray(1.0 - tri)
    invf = np.power(f32(500000.0), -(np.arange(0, 16, 2, dtype=f32) / f32(16))).astype(f32)
    invf = np.ascontiguousarray(np.broadcast_to(invf[None, :], (128, 8)))

    in_maps = []
    for c in range(NCORES):
        b, j = c // 4, c % 4
        t0 = j * TOK
        xs = np.ascontiguousarray(x[b, t0:t0 + TOK])
        if j == 0:
            xh = np.zeros((128, D), f32)
            ph = np.zeros((128,), np.int32)
        else:
            xh = np.ascontiguousarray(x[b, t0 - 128:t0])
            ph = positions[b, t0 - 128:t0]
        pos = np.concatenate([ph, positions[b, t0:t0 + TOK]]).reshape(17, 128).T
        flags = np.zeros((128, 3), f32)
        flags[:, :j] = 1.0
        in_maps.append({
            "x": xs, "xh": xh, "p": np.ascontiguousarray(p[0, b, t0:t0 + TOK]),
            "pos": np.ascontiguousarray(pos.astype(np.int32)),
            "w_in": w_in0, "w_out": w_out0, "w_pg": w_pg0, "w_pp": w_pp0,
            "nm": nm, "pn": pn, "gn": gn, "fn": fn, "sinks": sinks, "wgu": wgu,
            "ident": ident, "tri": tri, "tric": tric,
            "mprev0": tric if j > 0 else np.zeros((128, 128), f32),
            "flags": flags, "invf": invf,
        })
    nc = _get_nc()
    res = run_bass_kernel_spmd(nc, in_maps, core_ids=list(range(NCORES)))
    out = np.empty((2, 8192, D), f32)
    for c in range(NCORES):
        b, j = c // 4, c % 4
        out[b, j * TOK:(j + 1) * TOK] = np.asarray(res.results[c]["out"], f32)
    return out
```

```python
import numpy as np
from contextlib import ExitStack
import concourse.bass as bass
import concourse.mybir as mybir
from concourse.bass_utils import run_bass_kernel_spmd

F32 = mybir.dt.float32
BF16 = mybir.dt.bfloat16
I32 = mybir.dt.int32
AF = mybir.ActivationFunctionType
ALU = mybir.AluOpType

NCORES = 8
TOK = 2048
NT = 16
D = 1024
DIN = 5392
EPS = 1e-6
PI = float(np.float32(np.pi))
TWO_PI = float(np.float32(2 * np.pi))
CW1 = 6.28125
CW2 = float(np.float32(round((2 * np.pi - CW1) * 2 ** 20) / 2 ** 20))
CW3 = float(np.float32(2 * np.pi - CW1 - CW2))

C_AQ, C_AK, C_AV, C_GQ, C_GK, C_GV, C_GL, C_ZA, C_ZG = 0, 1024, 1152, 1280, 1792, 2304, 3328, 3344, 4368


class Sched:
    ENG = ("pe", "act", "dve", "pool", "sp")
    LIST_SCHEDULE = True

    def __init__(self):
        self.items = []
        self.alias = {}
        self.open_pe = None
        self.parity = 0
        self.prog = {e: [] for e in self.ENG}
        self.cnt = {e: 0 for e in self.ENG}
        self.seen = {e: {} for e in self.ENG}
        self.dma_cnt = {}

    def keys(self, names):
        out = []
        for n in names:
            if _DBG.get("noalias") and n in _DBG["noalias"]:
                ks = (n,)
            else:
                ks = self.alias.get(n, (n,))
            if _DBG.get("dbl") and n in _DBG["dbl"]:
                ks = tuple((k, self.parity) for k in ks)
            out.extend(ks)
        return out

    def op(self, eng, fn, reads=(), writes=(), inc=True, dma=None, cost=0.3, dma_inc=16):
        reads = self.keys(reads)
        writes = self.keys(writes)
        if eng == "pe":
            if self.open_pe is None:
                g = dict(eng="pe", fns=[], reads=[], writes=[], dma=None, cost=0.0)
                self.items.append(g)
                self.open_pe = g
            g = self.open_pe
            g["fns"].append(fn)
            g["reads"].extend(reads)
            g["writes"].extend(writes)
            g["cost"] += cost
            if inc:
                self.open_pe = None
            return
        assert self.open_pe is None, "non-PE op recorded inside an open PE group"
        self.items.append(dict(eng=eng, fns=[fn], reads=list(reads), writes=list(writes), dma=dma, cost=cost,
                               dma_inc=dma_inc))

    def barrier(self):
        assert self.open_pe is None
        self.items.append(("barrier",))

    @staticmethod
    def _is_bank(k):
        return isinstance(k, str) and len(k) == 2 and k[0] == "B" and k[1].isdigit()

    def _hazards(self, ops):
        lastw, readers = {}, {}
        for idx, o in enumerate(ops):
            preds = set()
            for r in o["reads"]:
                if r in lastw:
                    preds.add(lastw[r])
                if self._is_bank(r):
                    preds.update(readers.get(r, ()))
            for w in o["writes"]:
                if w in lastw:
                    preds.add(lastw[w])
                preds.update(readers.get(w, ()))
            preds.discard(idx)
            o["preds"] = preds
            for w in o["writes"]:
                lastw[w] = idx
                readers[w] = set()
            for r in o["reads"]:
                readers.setdefault(r, set()).add(idx)

    def _order(self, ops):
        n = len(ops)
        if not self.LIST_SCHEDULE:
            return list(range(n))
        LAT = 0.15
        succs = [[] for _ in range(n)]
        npred = [0] * n
        for i, o in enumerate(ops):
            npred[i] = len(o["preds"])
            for p in o["preds"]:
                succs[p].append(i)
        bl = [0.0] * n
        for i in range(n - 1, -1, -1):
            m = 0.0
            for sidx in succs[i]:
                if bl[sidx] + LAT > m:
                    m = bl[sidx] + LAT
            bl[i] = ops[i]["cost"] + m
        ready_t = [0.0] * n
        finish = [0.0] * n
        free = {e: 0.0 for e in self.ENG}
        ready = {e: [] for e in self.ENG}
        for i in range(n):
            if npred[i] == 0:
                ready[ops[i]["eng"]].append(i)
        picks = []
        done = 0
        mode = _DBG.get("sched", "bl")
        while done < n:
            best = None
            for e in self.ENG:
                lst = ready[e]
                if not lst:
                    continue
                fe = free[e]
                bi, bkey = None, None
                for i in lst:
                    t = ready_t[i] if ready_t[i] > fe else fe
                    if mode == "bl":
                        key = (t - fe if t - fe > 0.05 else 0.0, -bl[i], i)
                    else:
                        key = (t, i)
                    if bkey is None or key < bkey:
                        bi, bkey = i, key
                t = ready_t[bi] if ready_t[bi] > fe else fe
                if best is None or (t, bi) < (best[1], best[0]):
                    best = (bi, t, e)
            i, t, e = best
            ready[e].remove(i)
            o = ops[i]
            if o["dma"]:
                free[e] = t + 0.1
                finish[i] = t + o["cost"]
            else:
                free[e] = t + o["cost"]
                finish[i] = free[e]
            picks.append((t, i))
            done += 1
            for sidx in succs[i]:
                if finish[i] + LAT > ready_t[sidx]:
                    ready_t[sidx] = finish[i] + LAT
                npred[sidx] -= 1
                if npred[sidx] == 0:
                    ready[ops[sidx]["eng"]].append(sidx)
        self.est_time = getattr(self, "est_time", 0.0) + (max(finish) if n else 0.0)
        return [i for (_, i) in picks]

    def _emit_segment(self, ops):
        self._hazards(ops)
        order = self._order(ops)
        know = {}
        pos = {i: n for n, i in enumerate(order)}
        for i in order:
            o = ops[i]
            eng = o["eng"]
            own = None if o["dma"] else "S_" + eng
            sd = self.seen[eng]
            for p in sorted(o["preds"], key=lambda q: -pos[q]):
                sem, val = ops[p]["sv"]
                if sem == own and eng == "pe":
                    continue
                if sd.get(sem, 0) >= val:
                    continue
                self.prog[eng].append(("wait", sem, val))
                for s2, v2 in know[p].items():
                    if sd.get(s2, 0) < v2:
                        sd[s2] = v2
            if o["dma"]:
                di = o.get("dma_inc", 16)
                self.dma_cnt[o["dma"]] = self.dma_cnt.get(o["dma"], 0) + di
                sem, val, incn = o["dma"], self.dma_cnt[o["dma"]], di
            else:
                self.cnt[eng] += 1
                sem, val, incn = own, self.cnt[eng], 1
            o["sv"] = (sem, val)
            kn = dict(sd)
            kn[sem] = val
            know[i] = kn
            fns = o["fns"]
            for k, fn in enumerate(fns):
                self.prog[eng].append(("op", fn, sem, incn if k == len(fns) - 1 else 0))

    def _emit_barrier(self):
        tot = {("S_" + e): self.cnt[e] for e in ("pe", "act", "dve", "pool")}
        tot.update(self.dma_cnt)
        for e in self.ENG:
            for sem, val in tot.items():
                if val > 0 and sem != "S_" + e and self.seen[e].get(sem, 0) < val:
                    self.seen[e][sem] = val
                    self.prog[e].append(("wait", sem, val))

    def finalize(self):
        assert self.open_pe is None
        seg = []
        for it in self.items + [("barrier",)]:
            if isinstance(it, tuple):
                if seg:
                    self._emit_segment(seg)
                    seg = []
                self._emit_barrier()
            else:
                seg.append(it)


def _emit(nc, S, es):
    S.finalize()
    sem_names = ["S_pe", "S_act", "S_dve", "S_pool"] + sorted(S.dma_cnt)
    sems = {n: es.enter_context(nc.semaphore(n)) for n in sem_names}
    block = es.enter_context(nc.Block())

    def run(eng_name):
        def f(e):
            for it in S.prog[eng_name]:
                if it[0] == "wait":
                    e.wait_ge(sems[it[1]], it[2])
                else:
                    _, fn, sem, incn = it
                    ins = fn(e)
                    if incn:
                        ins.then_inc(sems[sem], incn)
        return f

    block.sync(run("sp"))
    block.tensor(run("pe"))
    block.scalar(run("act"))
    block.vector(run("dve"))
    block.gpsimd(run("pool"))
    return nc


_DBG = {"level": 9, "tiles": NT, "p1tiles": NT, "stop": None}


class _Stop(Exception):
    pass


def ck(n):
    if _DBG["stop"] == n:
        raise _Stop()


def build_nc():
    nc = bass.Bass("TRN2", target_bir_lowering=False)
    LVL = _DBG["level"]

    def din(name, shape, dt=F32):
        return nc.dram_tensor(name, list(shape), dt, kind="ExternalInput").ap()

    x_d = din("x", [TOK, D])
    xh_d = din("xh", [128, D])
    p_d = din("p", [TOK, 256])
    pos_d = din("pos", [128, 17], I32)
    win_d = din("w_in", [D, DIN])
    wout_d = din("w_out", [2048, D])
    wpg_d = din("w_pg", [D, D])
    wpp_d = din("w_pp", [256, D])
    nm_d = din("nm", [128, 8])
    pn_d = din("pn", [128, 8])
    gn_d = din("gn", [128, 2])
    fn_d = din("fn", [128, D])
    sinks_d = din("sinks", [128, 16])
    wgu_d = din("wgu", [17, 512])
    ident_d = din("ident", [128, 128])
    tri_d = din("tri", [128, 128])
    tric_d = din("tric", [128, 128])
    mprev0_d = din("mprev0", [128, 128])
    flags_d = din("flags", [128, 3])
    invf_d = din("invf", [128, 8])
    out_d = nc.dram_tensor("out", [TOK, D], F32, kind="ExternalOutput").ap()
    cc_in = nc.dram_tensor("cc_in", [128, 1040], F32)
    cc_out = nc.dram_tensor("cc_out", [512, 1040], F32)
    sc_e1 = nc.dram_tensor("sc_e1", [NT * 128, 512], F32).ap()
    sc_ki = nc.dram_tensor("sc_ki", [NT * 128, 512], BF16).ap()
    sc_ktl = nc.dram_tensor("sc_ktl", [NT * 128, 512], BF16).ap()
    sc_vg = nc.dram_tensor("sc_vg", [NT * 128, 1024], BF16).ap()

    S = Sched()
    es = ExitStack()
    with es:
        def sb(name, shape, dt):
            return es.enter_context(nc.sbuf_tensor(name, list(shape), dt))

        def psb(name, shape, dt):
            return es.enter_context(nc.psum_tensor(name, list(shape), dt))

        WIN = sb("WIN", [128, 8, DIN], BF16)
        WOUT = sb("WOUT", [128, 16, D], BF16)
        WPG = sb("WPG", [128, 8, D], BF16)
        WPP = sb("WPP", [128, 2, D], BF16)
        IDENT = sb("IDENT", [128, 128], F32)
        IDB = sb("IDB", [128, 128], BF16)
        TRI = sb("TRI", [128, 128], F32)
        TRIC = sb("TRIC", [128, 128], F32)
        MASK2 = sb("MASK2", [128, 2, 128], BF16)
        MASK0 = sb("MASK0", [128, 2, 128], BF16)
        FN = sb("FN", [128, D], F32)
        CC = sb("CC", [128, 17, 16], F32)
        SS = sb("SS", [128, 17, 16], F32)
        WGU = sb("WGU", [32, 512], F32)
        GLT = sb("GLT", [32, 128], F32)
        ESINK = sb("ESINK", [128, 16], F32)
        NM = sb("NM", [128, 8], F32)
        PN = sb("PN", [128, 8], F32)
        GN = sb("GN", [128, 2], F32)
        FLAGS = sb("FLAGS", [128, 3], F32)
        ONES = sb("ONES", [128, 2], F32)
        ST = sb("ST", [128, 32], F32)
        DLOG = sb("DLOG", [128, 4], F32)
        DEC = sb("DEC", [128, 4], F32)
        X = [sb("X0", [128, D], F32)]
        XN = sb("XN", [128, D], BF16)
        UT = sb("UT", [128, 8, 128], BF16)
        XNs = [XN, sb("XNb", [128, D], BF16)]
        UTs = [UT, sb("UTb", [128, 8, 128], BF16)]
        cur = {"UT": UT, "kUT": "UT"}
        K2 = sb("K2", [128, 2, 2, 64], BF16)
        KTD = [sb("KTD0", [128, 2, 128], BF16), sb("KTD1", [128, 2, 128], BF16)]
        VA = [sb("VA0", [128, 2, 65], BF16), sb("VA1", [128, 2, 65], BF16)]
        VG = sb("VG", [128, 4, 256], BF16)
        SA = sb("SA", [128, D], BF16)
        SG = sb("SG", [128, D], BF16)
        PT = [sb("PT%d" % i, [128, 2, 128], BF16) for i in range(4)]
        Y = sb("Y", [128, 2048], BF16)
        SST = sb("SST", [128, 4, 256], F32)
        SB = sb("SB", [128, 4, 256], BF16)
        PF = sb("PF", [128, 256], F32)
        GL = sb("GL", [128, 16], F32)
        RT = [sb("RT0", [128, 128], F32), sb("RT1", [128, 128], F32)]
        GL1 = PF[:, 128:144]
        GLT1 = PF[0:32, 0:128]
        DEC1 = PF[:, 144:148]
        DECALL = sb("DECALL", [128, NT, 4], F32)
        R1 = sb("R1", [128, 6400], F32)

        def r1(off, nbytes, dt):
            v = R1[:, off // 4:(off + nbytes) // 4]
            return v if dt == F32 else v.bitcast(dt)

        def r1keys(off, nbytes):
            return tuple(("R1", g) for g in range(off // 1024, (off + nbytes + 1023) // 1024))

        def r1buf(name, off, nbytes, dt):
            S.alias[name] = r1keys(off, nbytes)
            return r1(off, nbytes, dt)

        QT = r1buf("QT", 0, 2048, BF16)
        TS = [r1buf("TS0", 2048, 2048, F32), r1buf("TS1", 4096, 2048, F32)]
        E1 = r1buf("E1", 6144, 2048, F32)
        QD = r1buf("QD", 8192, 1024, BF16)
        QDT = r1buf("QDT", 9216, 1024, BF16)
        KIT = r1buf("KIT", 10240, 1024, BF16)
        KTL = r1buf("KTL", 11264, 1024, BF16)
        OG = r1buf("OG", 16384, 4096, F32)
        YT = r1buf("YT", 12288, 4096, BF16)
        HN = r1buf("HN", 12288, 2048, BF16)
        HNT = r1buf("HNT", 14336, 2048, BF16)
        H = r1buf("H", 16384, 4096, F32)
        SIG = r1buf("SIG", 20480, 4096, F32)
        OUT = r1buf("OUT", 20480, 4096, F32)
        PB = r1buf("PB", 24576, 512, BF16)
        PTT = r1buf("PTT", 25088, 512, BF16)
        S.alias["SIGa"] = r1keys(20480, 2048)
        S.alias["SIGb"] = r1keys(22528, 2048)
        P1LG = r1buf("P1LG", 4096, 2048, F32)
        P1E3 = r1buf("P1E3", 10240, 2048, F32)
        P1KTL = r1buf("P1KTL", 14336, 1024, BF16)
        STG = [r1buf("STG%d" % i, i * 6208, 6208, F32) for i in range(4)]
        GX = r1buf("GX", 0, 3 * 1040 * 4, F32)
        Yf = Y[:].bitcast(F32)
        SAf = SA[:].bitcast(F32)
        ROPE = [Yf[:, i * 136:(i + 1) * 136] for i in range(7)] + [SAf[:, 0:136]]
        ROPEI = SAf[:, 136:272].bitcast(I32)
        POSI = SAf[:, 272:289].bitcast(I32)
        MP0 = SAf[:, 384:512]
        LGb = r1buf("LGb", 0, 2048, F32)
        E3b = r1buf("E3b", 2048, 2048, F32)
        KTLb = r1buf("KTLb", 6144, 1024, BF16)
        E2b = r1buf("E2b", 7168, 2048, F32)
        KIa = r1buf("KIa", 12288, 1024, BF16)
        KIb = r1buf("KIb", 13312, 1024, BF16)
        CCS = r1buf("CCS", 0, 4160, F32)
        S.alias["INVF"] = ("ROPE0",)
        for a_, b_ in (("XN0", "XN"), ("XN1", "XNb"), ("UTp0", "UT"), ("UTp1", "UTb"),
                       ("ss0_0", "p1ss0"), ("rs0_0", "p1rs0"), ("ss0_1", "p1ss1"), ("rs0_1", "p1rs1")):
            S.alias[a_] = (b_,)
        S.alias["YTa"] = r1keys(12288, 2048)
        S.alias["YTb"] = r1keys(14336, 2048)
        for nm_, bk in (("B2a", "B2"), ("B2b", "B2"), ("B3a", "B3"), ("B3b", "B3"), ("B3c", "B3"), ("B3d", "B3"),
                        ("B4L", "B4"), ("B4R", "B4"), ("B5L", "B5"), ("B5R", "B5"), ("B6L", "B6"), ("B6R", "B6")):
            S.alias[nm_] = (bk,)
        if _DBG.get("vtp"):
            S.alias["B2a"] = ("VTa",)
            S.alias["B2b"] = ("VTb",)
        S.alias["ssg"] = ("ssg0", "ssg1", "ssg2", "ssg3")
        for h in range(4):
            S.alias["OG%d" % h] = r1keys(16384 + h * 1024, 1024)
            S.alias["OGx%d" % h] = r1keys(16384 + h * 1024, 1024)
        S.alias["Yall"] = ("Ya0", "Ya1", "Ya2", "Ya3", "Yg0", "Yg1", "Yg2", "Yg3")

        BK = [psb("B%d" % i, [128, 512], F32) if i != 2 else None for i in range(8)]
        B2bf = psb("B2", [128, 1024], BF16)[:]
        B2f32 = B2bf.bitcast(F32)

        def fsz(ap):
            n = 1
            for d in ap.shape[1:]:
                n *= d
            return n

        def ecost(eng, ap):
            n = fsz(ap)
            if eng == "act":
                return n / 1200.0 + 0.22
            if eng == "dve":
                return n / 960.0 + 0.12
            return n / 500.0 + 0.3

        def dma(out, in_, sem, reads=(), writes=(), eng="sp"):
            nb = fsz(out) * out.shape[0] * 4
            return S.op(eng, lambda e, o=out, i=in_: e.dma_start(out=o, in_=i), reads, writes, dma=sem,
                        cost=2.0 + nb / 150e3)

        def dmaT(out, in_, sem, reads=(), writes=()):
            nb = fsz(out) * out.shape[0] * 2
            return S.op("sp", lambda e, o=out, i=in_: e.dma_start_transpose(out=o, in_=i), reads, writes, dma=sem,
                        cost=2.5 + nb / 150e3)

        def mm(out, lhsT, rhs, start, stop, reads, writes, inc=None):
            inc = stop if inc is None else inc
            c = fsz(rhs) * (4 if lhsT.dtype == F32 else 1) / _DBG.get("pe_rate", 1800.0) + _DBG.get("pe_fix", 0.12)
            return S.op("pe", lambda e, o=out, l=lhsT, r=rhs, s=start, t=stop:
                        e.matmul(out=o, lhsT=l, rhs=r, start=s, stop=t), reads, writes, inc=inc, cost=c)

        def tp(out, in_, ident, reads, writes, inc=True):
            return S.op("pe", lambda e, o=out, i=in_, d=ident: e.transpose(out=o, in_=i, identity=d),
                        reads, writes, inc=inc, cost=0.1)

        def act(out, in_, func, reads, writes, scale=None, bias=None, accum=None):
            kw = {}
            if scale is not None:
                kw["scale"] = scale
            if bias is not None:
                kw["bias"] = bias
            if accum is not None:
                kw["accum_out"] = accum
            return S.op("act", lambda e, o=out, i=in_, f=func, k=kw: e.activation(out=o, in_=i, func=f, **k),
                        reads, writes, cost=ecost("act", out))

        def tt(eng, out, in0, in1, op, reads, writes):
            return S.op(eng, lambda e, o=out, a=in0, b=in1, p=op: e.tensor_tensor(out=o, in0=a, in1=b, op=p),
                        reads, writes, cost=ecost(eng, out))

        def tsc(eng, out, in0, s1, op0, reads, writes, s2=None, op1=None):
            if op1 is None:
                return S.op(eng, lambda e, o=out, a=in0, s=s1, p=op0:
                            e.tensor_scalar(out=o, in0=a, scalar1=s, scalar2=None, op0=p), reads, writes,
                            cost=ecost(eng, out))
            return S.op(eng, lambda e, o=out, a=in0, s=s1, p=op0, s_2=s2, p1=op1:
                        e.tensor_scalar(out=o, in0=a, scalar1=s, scalar2=s_2, op0=p, op1=p1), reads, writes,
                        cost=ecost(eng, out))

        def stt(eng, out, in0, scalar, in1, op0, op1, reads, writes):
            return S.op(eng, lambda e, o=out, a=in0, s=scalar, b=in1, p0=op0, p1=op1:
                        e.scalar_tensor_tensor(out=o, in0=a, scalar=s, in1=b, op0=p0, op1=p1), reads, writes,
                        cost=ecost(eng, out))

        def cp(eng, out, in_, reads, writes):
            if eng == "act":
                return act(out, in_, AF.Copy, reads, writes)
            return S.op(eng, lambda e, o=out, i=in_: e.tensor_copy(out=o, in_=i), reads, writes,
                        cost=ecost(eng, out))

        def memset(eng, ap, val, writes):
            return S.op(eng, lambda e, a=ap, v=val: e.memset(a, v), (), writes, cost=ecost(eng, ap))

        def rstd_from_ss(ss_ap, out_ap, n, key_ss, key_out):
            act(out_ap, ss_ap, AF.Ln, [key_ss], [key_out], scale=1.0 / n, bias=EPS)
            act(out_ap, out_ap, AF.Exp, [key_out], [key_out], scale=-0.5)

        dma(IDENT[:], ident_d, "d_c0", writes=["IDENT"])
        dma(TRI[:], tri_d, "d_c1", writes=["TRI"])
        dma(TRIC[:], tric_d, "d_c2", writes=["TRIC"])
        dma(MP0, mprev0_d, "d_c3", writes=["MP0"])
        dma(ESINK[:], sinks_d, "d_c5", writes=["ESINK"])
        dma(NM[:], nm_d, "d_c6", writes=["NM"])
        dma(PN[:], pn_d, "d_c7", writes=["PN"])
        dma(GN[:], gn_d, "d_c8", writes=["GN"])
        dma(FLAGS[:], flags_d, "d_c9", writes=["FLAGS"])
        dma(WGU[0:17, :], wgu_d, "d_c10", writes=["WGU"])
        dma(POSI, pos_d, "d_c11", writes=["POSI"])
        dma(ROPE[0][:, 0:8], invf_d, "d_c12", writes=["INVF"])
        if LVL < 0.15:
            S.barrier()
            return _emit(nc, S, es)
        cp("dve", IDB[:], IDENT[:], ["IDENT"], ["IDB"])
        cp("dve", MASK2[:, 0, :], TRIC[:], ["TRIC"], ["MASK2"])
        cp("dve", MASK2[:, 1, :], TRI[:], ["TRI"], ["MASK2"])
        cp("dve", MASK0[:, 0, :], MP0, ["MP0"], ["MASK0"])
        cp("dve", MASK0[:, 1, :], TRI[:], ["TRI"], ["MASK0"])
        memset("dve", ONES[:], 1.0, ["ONES"])
        memset("dve", GLT[:], 1.0, ["GLT"])
        memset("dve", GLT1, 1.0, ["GLT1"])
        memset("dve", DLOG[:], 0.0, ["DLOG"])
        memset("dve", SST[:], 0.0, ["SST"])
        memset("dve", SB[:], 0.0, ["SB"])
        for b in range(2):
            memset("dve", VA[b][:], 1.0, ["VA%d" % b])
            memset("dve", KTD[b][:], 0.0, ["KTD%d" % b])
        act(ESINK[:], ESINK[:], AF.Exp, ["ESINK"], ["ESINK"])

        if LVL < 0.25:
            S.barrier()
            return _emit(nc, S, es)
        INVF = ROPE[0][:, 0:8]
        POSF = ROPE[1][:, 0:17]
        ANG = ROPE[2][:, 0:136].rearrange("p (t f) -> p t f", f=8)
        KF = ROPE[3][:, 0:136].rearrange("p (t f) -> p t f", f=8)
        RR = ROPE[4][:, 0:136].rearrange("p (t f) -> p t f", f=8)
        R2 = ROPE[5][:, 0:136].rearrange("p (t f) -> p t f", f=8)
        MM_ = ROPE[6][:, 0:136].rearrange("p (t f) -> p t f", f=8)
        SINV = ROPE[7][:, 0:136].rearrange("p (t f) -> p t f", f=8)
        KI32 = ROPEI[:, 0:136].rearrange("p (t f) -> p t f", f=8)
        cp("dve", POSF, POSI, ["POSI"], ["ROPE1"])
        tt("dve", ANG, POSF.unsqueeze(2).broadcast_to([128, 17, 8]),
           INVF.unsqueeze(1).broadcast_to([128, 17, 8]), ALU.mult, ["ROPE1", "ROPE0"], ["ROPE2"])
        tsc("dve", KI32, ANG, 1.0 / (2 * np.pi), ALU.mult, ["ROPE2"], ["ROPEI"])
        cp("dve", KF, KI32, ["ROPEI"], ["ROPE3"])
        stt("dve", RR, KF, -CW1, ANG, ALU.mult, ALU.add, ["ROPE3", "ROPE2"], ["ROPE4"])
        stt("dve", RR, KF, -CW2, RR, ALU.mult, ALU.add, ["ROPE3", "ROPE4"], ["ROPE4"])
        stt("dve", RR, KF, -CW3, RR, ALU.mult, ALU.add, ["ROPE3", "ROPE4"], ["ROPE4"])
        tsc("dve", RR, RR, PI, ALU.min, ["ROPE4"], ["ROPE4"], s2=-PI, op1=ALU.max)
        act(SINV, RR, AF.Sin, ["ROPE4"], ["ROPE7"])
        tsc("dve", R2, RR, float(np.pi / 2), ALU.add, ["ROPE4"], ["ROPE5"])
        tsc("dve", MM_, R2, PI, ALU.is_gt, ["ROPE5"], ["ROPE6"], s2=TWO_PI, op1=ALU.mult)
        tt("dve", R2, R2, MM_, ALU.subtract, ["ROPE5", "ROPE6"], ["ROPE5"])
        tsc("dve", R2, R2, PI, ALU.min, ["ROPE5"], ["ROPE5"], s2=-PI, op1=ALU.max)
        act(CC[:, :, 0:8], R2, AF.Sin, ["ROPE5"], ["CC"])
        cp("dve", CC[:, :, 8:16], CC[:, :, 0:8], ["CC"], ["CC"])
        cp("dve", SS[:, :, 8:16], SINV, ["ROPE7"], ["SS"])
        tsc("dve", SS[:, :, 0:8], SINV, -1.0, ALU.mult, ["ROPE7"], ["SS"])

        STG2 = [r1buf("STGa", 15360, 5120, F32), r1buf("STGb", 20480, 5120, F32)]
        wstate = {"i": 0}

        def wpiece(src, n, dst, sc, slots, tag, engs):
            i = wstate["i"]
            wstate["i"] += 1
            sl = i % len(slots)
            dma(slots[sl][:, 0:n], src, "d_%s%d" % (tag, sl), writes=["%s%d" % (tag, sl)])
            eng = engs[i % len(engs)]
            rd = ["%s%d" % (tag, sl), "NM", "PN", "GN"]
            wk = [("Wp", i)]
            if sc is None:
                cp(eng, dst, slots[sl][:, 0:n], rd, wk)
            elif eng == "act":
                act(dst, slots[sl][:, 0:n], AF.Copy, rd, wk, scale=sc)
            else:
                tsc(eng, dst, slots[sl][:, 0:n], sc, ALU.mult, rd, wk)

        S.alias["STGa0"] = S.alias["STGa"]
        S.alias["STGa1"] = S.alias["STGb"]
        if LVL >= 2:
            for c in range(8):
                wpiece(win_d[c * 128:(c + 1) * 128, C_GK:C_ZA], C_ZA - C_GK, WIN[:, c, C_GK:C_ZA], NM[:, c:c + 1],
                       STG, "STG", ["dve", "act"])
        pieces2 = []
        for c in range(8):
            for (c0, c1) in ((0, 1280), (1280, C_GK), (C_ZA, C_ZA + 1280), (C_ZA + 1280, DIN)):
                pieces2.append((win_d[c * 128:(c + 1) * 128, c0:c1], c1 - c0, WIN[:, c, c0:c1], NM[:, c:c + 1]))
        for c in range(16):
            sc = None if c < 8 else GN[:, (c % 2):(c % 2) + 1]
            pieces2.append((wout_d[c * 128:(c + 1) * 128, :], 1024, WOUT[:, c, :], sc))
        for c in range(8):
            pieces2.append((wpg_d[c * 128:(c + 1) * 128, :], 1024, WPG[:, c, :], PN[:, c:c + 1]))
        for c in range(2):
            pieces2.append((wpp_d[c * 128:(c + 1) * 128, :], 1024, WPP[:, c, :], None))
        if LVL < 2:
            pieces2 = []

        def stream_weights(k):
            for _ in range(k):
                if pieces2:
                    src, n, dst, sc = pieces2.pop(0)
                    wpiece(src, n, dst, sc, STG2, "STGa", ["act", "act", "dve"])
        S.barrier()

        def load_x(xsrc, xb=0):
            dma(X[0][:], xsrc, "d_x0", writes=["X0"])

        def stage_A(xsrc, par, xbuf=None, kx="X0"):
            xb_ = X[0][:] if xbuf is None else xbuf
            XNp, UTp = XNs[par], UTs[par]
            kXN, kUT = "XN%d" % par, "UTp%d" % par
            so = 0 if par == 0 else 6
            kss, krs = "ss0_%d" % par, "rs0_%d" % par
            act(XNp[:], xb_, AF.Square, [kx], [kXN, kss], accum=ST[:, so:so + 1])
            rstd_from_ss(ST[:, so:so + 1], ST[:, so + 1:so + 2], D, kss, krs)
            tsc("dve", XNp[:], xb_, ST[:, so + 1:so + 2], ALU.mult, [kx, krs], [kXN])
            dmaT(UTp[:], XNp[:].rearrange("p (c t) -> p c t", c=8), "d_tu%d" % par, [kXN], [kUT])

        def use_ut(par):
            cur["UT"], cur["kUT"] = UTs[par], "UTp%d" % par

        pbank = [0]

        def proj(col0, ncols, lhs, lhs_key, W, nk, wcol0=None):
            b = (0, 1, 3, 2)[pbank[0] % 4]
            pbank[0] += 1
            key = "B%d" % b
            if _DBG.get("vproj"):
                pbank.append(0)
                key = "VP%d" % (len(pbank) % _DBG["vproj"])
            ps = (BK[b][:] if b != 2 else B2f32)[:, 0:ncols]
            for k in range(nk):
                if isinstance(lhs_key, list):
                    half = nk // len(lhs_key)
                    lk = lhs_key[k // half]
                    ginc = (k % half == half - 1)
                else:
                    lk, ginc = lhs_key, (k == nk - 1 or (k % 4 == 3))
                mm(ps, lhs[:, k, :], W[:, k, col0:col0 + ncols], k == 0, k == nk - 1,
                   [lk, "W"], [key], inc=ginc)
            return ps, key

        def gla_kv_state(h, t):
            hk = "B4%s" % "LR"[h % 2]
            ps = BK[4][:, (h % 2) * 256:(h % 2) * 256 + 256]
            mm(ps, KTL[:, h * 128:(h + 1) * 128], VG[:, h, :], True, True, ["KTL", "VG"], [hk])
            tsc("dve", SST[:, h, :], SST[:, h, :], DECALL[:, t, h:h + 1], ALU.mult, ["SST%d" % h], ["SST%d" % h])
            stt("dve", SST[:, h, :], ps, DECALL[:, t, h:h + 1], SST[:, h, :], ALU.mult, ALU.add,
                [hk, "SST%d" % h], ["SST%d" % h])

        p1bank = [0]
        P1_BANKS = [0, 1, 6, 7, 2]

        def p1_proj(col0, ncols, lhs, lhs_key):
            b = P1_BANKS[p1bank[0] % len(P1_BANKS)]
            p1bank[0] += 1
            key = "B%d" % b
            ps = (BK[b][:] if b != 2 else B2f32)[:, 0:ncols]
            for k in range(8):
                mm(ps, lhs[:, k, :], WIN[:, k, col0:col0 + ncols], k == 0, k == 7, [lhs_key, "W"], [key])
            return ps, key

        def p1_tile(t):
            par = (t + 1) % 2
            sfx = "" if par == 0 else "b"
            XNp = XNs[par][:]
            UTp = UTs[par][:]
            VGp = VG[:] if par == 0 else Y[:, 0:1024].rearrange("p (h d) -> p h d", h=4)
            E1p = (SA if par == 0 else SG)[:].bitcast(F32)
            E2p = Y[:, 1024:2048].bitcast(F32) if par == 0 else E2b
            KIp = KIa if par == 0 else KIb
            kE1, kE2, kKI = ("SA", "Yp1b", "KIa") if par == 0 else ("SG", "E2b", "KIb")
            LGp, E3p, KTLp = (P1LG, P1E3, P1KTL) if par == 0 else (LGb, E3b, KTLb)
            GLp, GLTp, DECp = (GL, GLT, DEC) if par == 0 else (GL1, GLT1, DEC1)
            kXN, kUT, kVG = ("XN", "UT", "VG") if par == 0 else ("XNb", "UTb", "Yp1")
            kLG, kE3, kKTL = ("P1LG", "P1E3", "P1KTL") if par == 0 else ("LGb", "E3b", "KTLb")
            kGL, kGLT, kDEC = "GL%d" % par, "GLT%d" % par, "DEC%d" % par
            so = 0 if par == 0 else 6
            kss, krs = "p1ss%d" % par, "p1rs%d" % par
            kx = "X0"
            act(XNp, X[0][:], AF.Square, [kx], [kXN, kss], accum=ST[:, so:so + 1])
            rstd_from_ss(ST[:, so:so + 1], ST[:, so + 1:so + 2], D, kss, krs)
            tsc("dve", XNp, X[0][:], ST[:, so + 1:so + 2], ALU.mult, [kx, krs], [kXN])
            if t + 1 < _DBG["p1tiles"]:
                load_x(x_d[(t + 1) * 128:(t + 2) * 128, :])
            dmaT(UTp, XNp.rearrange("p (c t) -> p c t", c=8), "d_tu%d" % par, [kXN], [kUT])
            for k in range(8):
                mm(BK[3][0:16, 0:128], WIN[:, k, C_GL:C_GL + 16], UTp[:, k, :], k == 0, k == 7, [kUT, "W"], ["B3"])
            cp("dve", GLTp[0:16, :], BK[3][0:16, 0:128], ["B3"], [kGLT])
            mm(BK[4][:], GLTp[0:17, :], WGU[0:17, :], True, True, [kGLT, "WGU"], ["B4"])
            act(LGp, BK[4][:], AF.Exp, ["B4"], [kLG], scale=-1.0)
            act(LGp, LGp, AF.Ln, [kLG], [kLG], bias=1.0)
            LHI = E3p.bitcast(BF16)[:, 0:512]
            LLO = E3p.bitcast(BF16)[:, 512:1024]
            cp("act", LHI, LGp, [kLG], [kE3 + "h"])
            tt("dve", LLO, LGp, LHI, ALU.subtract, [kLG, kE3 + "h"], [kE3 + "l"])
            for h in range(4):
                mm(BK[3][:, 128 + h:129 + h], LGp[:, h * 128:(h + 1) * 128], ONES[:, 0:1], True, True,
                   [kLG, "ONES"], ["B3"], inc=(h == 3))
            mm(BK[4][:], MASK2[:, 1, :], LHI, True, False, ["MASK2", kE3 + "h"], ["B4"], inc=False)
            mm(BK[4][:], MASK2[:, 1, :], LLO, False, True, ["MASK2", kE3 + "l"], ["B4"], inc=True)
            tt("dve", DLOG[:], DLOG[:], BK[3][:, 128:132], ALU.add, ["DLOG", "B3"], ["DLOG"])
            act(DECp[:], BK[3][:, 128:132], AF.Exp, ["B3"], [kDEC], scale=-1.0 / 16)
            cp("pool", DECALL[:, t, :], DECp[:], [kDEC], [("DECALL", t)])
            act(E1p, BK[4][:], AF.Exp, ["B4"], [kE1], scale=-1.0 / 16)
            act(E2p, BK[4][:], AF.Exp, ["B4"], [kE2], scale=1.0 / 16)
            rows = slice(t * 128, (t + 1) * 128)
            dma(sc_e1[rows, :], E1p, "d_s1%d" % par, reads=[kE1], writes=[("sc_e1", t)])
            ps, key = p1_proj(C_GK, 512, UTp, kUT)
            tt("dve", KIp, ps, E2p, ALU.mult, [key, kE2], [kKI])
            dma(sc_ki[rows, :], KIp, "d_s3%d" % par, reads=[kKI], writes=[("sc_ki", t)])
            for hf in range(2):
                ps, key = p1_proj(C_GV + hf * 512, 512, UTp, kUT)
                cp("act", VGp[:, hf * 2:hf * 2 + 2, :], ps.rearrange("p (h d) -> p h d", h=2), [key], [kVG])
            dma(sc_vg[rows, :], VGp.rearrange("p h d -> p (h d)"), "d_s4%d" % par, reads=[kVG], writes=[("sc_vg", t)])
            for h in range(4):
                ps = BK[5][:, (h % 2) * 256:(h % 2) * 256 + 256]
                mm(ps, KIp[:, h * 128:(h + 1) * 128], VGp[:, h, :], True, True, [kKI, kVG], ["B5"])
                tsc("dve", SST[:, h, :], SST[:, h, :], DECp[:, h:h + 1], ALU.mult, ["SST%d" % h, kDEC], ["SST%d" % h])
                stt("dve", SST[:, h, :], ps, DECp[:, h:h + 1], SST[:, h, :], ALU.mult, ALU.add,
                    ["B5", kDEC, "SST%d" % h], ["SST%d" % h])

        def _phases():
            P1T = _DBG["p1tiles"] if LVL >= 3 else 0
            if P1T:
                load_x(x_d[0:128, :], 0)
            for t in range(P1T):
                stream_weights(4)
                p1_tile(t)
            stream_weights(1000)
            P2T = _DBG["tiles"]
            if LVL >= 5:
                load_x(xh_d)
                dma(FN[:], x_d[0:128, :], "d_xf", writes=["FN"])
                stage_A(xh_d, 1)
                if P2T > 1:
                    load_x(x_d[128:256, :])
                stage_A(x_d[0:128, :], 0, xbuf=FN[:], kx="FN")
            if LVL >= 4:
                act(DLOG[:], DLOG[:], AF.Exp, ["DLOG"], ["DLOG"], scale=-1.0 / 16)
                S.barrier()
                dma(FN[:], fn_d, "d_c4", writes=["FN"])
                CCSx = r1buf("CCSx", 12288, 4160, F32)
                GXx = r1buf("GXx", 12288, 3 * 1040 * 4, F32)
                CCS3 = CCSx.rearrange("p (h c) -> p h c", h=4)
                memset("dve", CCSx, 0.0, ["CCSx"])
                cp("dve", CCS3[:, :, 0:256], SST[:], ["SST0", "SST1", "SST2", "SST3", "CCSx"], ["CCSx"])
                cp("dve", CCS3[:, :, 256], DLOG[:], ["DLOG", "CCSx"], ["CCSx"])
                dma(cc_in.ap(), CCSx, "d_cc", reads=["CCSx"], writes=["cc_in"])
                S.op("pool", lambda e: e.collective_compute(
                    "AllGather", ALU.bypass, replica_groups=[[0, 1, 2, 3], [4, 5, 6, 7]],
                    ins=[cc_in.ap().opt()], outs=[cc_out.ap().opt()]), ["cc_in"], ["cc_out"], cost=45.0,
                    dma="d_coll", dma_inc=1)
                GXv = GXx.rearrange("p (r h c) -> p r h c", r=3, h=4)
                dma(GXx.rearrange("p (r c) -> p r c", r=3), cc_out.ap()[0:384, :].rearrange("(r p) c -> p r c", p=128),
                    "d_cc2", reads=["cc_out"], writes=["GXx"])
                AR = ST[:, 8:20].rearrange("p (r h) -> p r h", r=3)
                for r in range(3):
                    tsc("dve", AR[:, r, :], GXv[:, r, :, 256], FLAGS[:, r:r + 1], ALU.mult, ["GXx", "FLAGS"], ["AR"],
                        s2=FLAGS[:, r:r + 1], op1=ALU.subtract)
                    tsc("dve", AR[:, r, :], AR[:, r, :], 1.0, ALU.add, ["AR"], ["AR"])
                for r in (1, 2):
                    tsc("dve", GXv[:, r, :, 0:256], GXv[:, r, :, 0:256], FLAGS[:, r:r + 1], ALU.mult,
                        ["GXx", "FLAGS"], ["GXx"])
                for h in range(4):
                    tsc("dve", SST[:, h, :], GXv[:, 0, h, 0:256], FLAGS[:, 0:1], ALU.mult, ["GXx", "FLAGS"], ["SST%d" % h])
                    for r in (1, 2):
                        stt("dve", SST[:, h, :], SST[:, h, :], AR[:, r, h:h + 1], GXv[:, r, h, 0:256],
                            ALU.mult, ALU.add, ["SST%d" % h, "AR", "GXx"], ["SST%d" % h])
                    cp("act", SB[:, h, :], SST[:, h, :], ["SST%d" % h], ["SB%d" % h])

            def rope_evac(ps, key, nh, dst3, dst_key, tt_idx, dup):
                src = ps.rearrange("p (h d) -> p h d", h=nh)
                cc = CC[:, tt_idx, :].unsqueeze(1).broadcast_to([128, nh, 16])
                ssl = SS[:, tt_idx, 0:8].unsqueeze(1).broadcast_to([128, nh, 8])
                ssh = SS[:, tt_idx, 8:16].unsqueeze(1).broadcast_to([128, nh, 8])
                TA = RT[0][:, 0:nh * 16].rearrange("p (h d) -> p h d", h=nh)
                TB = RT[1][:, 0:nh * 16].rearrange("p (h d) -> p h d", h=nh)
                tt("dve", TA, src[:, :, 0:16], cc, ALU.mult, [key, "CC"], ["RT0"])
                tt("dve", TB[:, :, 0:8], src[:, :, 8:16], ssl, ALU.mult, [key, "SS"], ["RT1"])
                tt("dve", TB[:, :, 8:16], src[:, :, 0:8], ssh, ALU.mult, [key, "SS"], ["RT1"])
                if dup:
                    for cpy in range(2):
                        tt("dve", dst3[:, :, cpy, 0:16], TA, TB, ALU.add, ["RT0", "RT1"], [dst_key])
                        cp("act", dst3[:, :, cpy, 16:64], src[:, :, 16:64], [key], [dst_key])
                else:
                    tt("dve", dst3[:, :, 0:16], TA, TB, ALU.add, ["RT0", "RT1"], [dst_key])
                    cp("act", dst3[:, :, 16:64], src[:, :, 16:64], [key], [dst_key])

            def kv_attn_proj(tt_idx, kb):
                ps, key = proj(C_AK, 256, cur["UT"], cur["kUT"], WIN, 8)
                rope_evac(ps[:, 0:128], key, 2, K2[:], "K2", tt_idx, True)
                cp("act", VA[kb][:, :, 0:64], ps[:, 128:256].rearrange("p (g d) -> p g d", g=2), [key], ["VA%d" % kb])
                dmaT(KTD[kb][:], K2[:].rearrange("p g c d -> p g (c d)"), "d_tk%d" % kb, ["K2"], ["KTD%d" % kb])

            def silu_evac(col0, dst, dst_key):
                for hf in range(2):
                    ps, key = proj(col0 + hf * 512, 512, cur["UT"], cur["kUT"], WIN, 8)
                    tk = "TS%d" % hf
                    act(TS[hf], ps, AF.Exp, [key], [tk], scale=-1.0)
                    act(TS[hf], TS[hf], AF.Ln, [tk], [tk], bias=1.0)
                    act(TS[hf], TS[hf], AF.Exp, [tk], [tk], scale=-1.0)
                    tt("dve", dst[:, hf * 512:(hf + 1) * 512], ps, TS[hf], ALU.mult, [key, tk], [dst_key])

            def attention(t, kb):
                pb = kb ^ 1
                mask = MASK0 if t == 0 else MASK2
                mkey = "MASK0" if t == 0 else "MASK2"
                for sg in range(4):
                    g = sg // 2
                    obk = 7
                    okey = "B%d" % obk
                    for i in range(4):
                        h = sg * 4 + i
                        c, r0 = h // 2, (h % 2) * 64
                        sbk = (6, 4)[h % 2]
                        sk = "B%d" % sbk
                        sps = BK[sbk][:, 0:256]
                        pi = h % 4
                        mm(sps[:, 0:128], KTD[pb][r0:r0 + 64, g, :], QT[r0:r0 + 64, c * 128:(c + 1) * 128], True, True,
                           ["KTD%d" % pb, "QT"], [sk], inc=False)
                        mm(sps[:, 128:256], KTD[kb][r0:r0 + 64, g, :], QT[r0:r0 + 64, c * 128:(c + 1) * 128], True, True,
                           ["KTD%d" % kb, "QT"], [sk], inc=True)
                        pk = "PT%d" % pi
                        act(PT[pi][:].rearrange("p a q -> p (a q)"), sps, AF.Exp, [sk], [pk], scale=0.125)
                        tt("dve", PT[pi][:], PT[pi][:], mask[:], ALU.mult, [pk, mkey], [pk])
                        ops = BK[obk][:, i * 65:(i + 1) * 65]
                        mm(ops, PT[pi][:, 0, :], VA[pb][:, g, :], True, False, [pk, "VA%d" % pb], [okey], inc=False)
                        mm(ops, PT[pi][:, 1, :], VA[kb][:, g, :], False, True, [pk, "VA%d" % kb], [okey], inc=True)
                    o3 = BK[obk][:, 0:260].rearrange("p (h d) -> p h d", h=4)
                    DEN = ST[:, 20:24]
                    tt("dve", DEN, o3[:, :, 64], ESINK[:, sg * 4:(sg + 1) * 4], ALU.add, [okey, "ESINK"], ["DEN"])
                    S.op("dve", lambda e, o=DEN: e.reciprocal(out=o, in_=o), ["DEN"], ["DEN"], cost=0.2)
                    ysl = Y[:, sg * 256:(sg + 1) * 256]
                    tt("dve", ysl.rearrange("p (h d) -> p h d", h=4), o3[:, :, 0:64],
                       DEN.unsqueeze(2).broadcast_to([128, 4, 64]), ALU.mult, [okey, "DEN"], ["Ya%d" % sg])
                    tt("pool", ysl, ysl, SA[:, sg * 256:(sg + 1) * 256], ALU.mult, ["Ya%d" % sg, "SA"], ["Ya%d" % sg])

            def gla(t):
                for h in range(4):
                    abk = 6
                    ak = "B%d" % abk
                    aps = BK[abk][:, 256:384]
                    mm(aps, KIT[:, h * 128:(h + 1) * 128], QDT[:, h * 128:(h + 1) * 128], True, True, ["KIT", "QDT"], [ak])
                    AT = PT[h % 2][:, 0, :]
                    tt("dve", AT, aps, TRI[:], ALU.mult, [ak, "TRI"], ["PT%d" % (h % 2)])
                    obk = 7 if h % 2 == 0 else 5
                    ok = "B%d" % obk
                    ops = BK[obk][:, 0:256]
                    mm(ops, AT, VG[:, h, :], True, False, ["PT%d" % (h % 2), "VG"], [ok], inc=False)
                    mm(ops, QDT[:, h * 128:(h + 1) * 128], SB[:, h, :], False, True, ["QDT", "SB%d" % h], [ok], inc=True)
                    ysl = Y[:, 1024 + h * 256:1024 + (h + 1) * 256]
                    act(ysl, ops, AF.Square, [ok], ["Yg%d" % h, "ssg%d" % h], accum=ST[:, 24 + h:25 + h])
                    rstd_from_ss(ST[:, 24 + h:25 + h], ST[:, 28 + h:29 + h], 256, "ssg%d" % h, "rsg%d" % h)
                    stt("dve", ysl, ops, ST[:, 28 + h:29 + h], SG[:, h * 256:(h + 1) * 256], ALU.mult, ALU.mult,
                        [ok, "rsg%d" % h, "SG"], ["Yg%d" % h])
                    gla_kv_state(h, t)
                    cp("pool", SB[:, h, :], SST[:, h, :], ["SST%d" % h], ["SB%d" % h])

            def stage_F(t, xb):
                dma(PF[:], p_d[t * 128:(t + 1) * 128, :], "d_p", writes=["PF"])
                YT3w = YT.rearrange("p (c t) -> p c t", c=16)
                Y3 = Y[:].rearrange("p (c t) -> p c t", c=16)
                dmaT(YT3w[:, 0:8, :], Y3[:, 0:8, :], "d_ty0", ["Ya0", "Ya1", "Ya2", "Ya3"], ["YTa"])
                dmaT(YT3w[:, 8:16, :], Y3[:, 8:16, :], "d_ty1", ["Yg0", "Yg1", "Yg2", "Yg3"], ["YTb"])
                YT3 = YT.rearrange("p (c t) -> p c t", c=16)
                dma(H, x_d[t * 128:(t + 1) * 128, :], "d_xr", writes=["H"])
                for hf in range(2):
                    ps, key = proj(hf * 512, 512, YT3, ["YTa", "YTb"], WOUT, 16)
                    tt("dve", H[:, hf * 512:(hf + 1) * 512], ps, H[:, hf * 512:(hf + 1) * 512], ALU.add,
                       [key, "H"], ["H"])
                act(HN, H, AF.Square, ["H"], ["HN", "ss1"], accum=ST[:, 2:3])
                rstd_from_ss(ST[:, 2:3], ST[:, 3:4], D, "ss1", "rs1")
                tsc("dve", HN, H, ST[:, 3:4], ALU.mult, ["H", "rs1"], ["HN"])
                dmaT(HNT.rearrange("p (c t) -> p c t", c=8), HN.rearrange("p (c t) -> p c t", c=8), "d_th", ["HN"], ["HNT"])
                HNT3 = HNT.rearrange("p (c t) -> p c t", c=8)
                for hf in range(2):
                    ps, key = proj(hf * 512, 512, HNT3, "HNT", WPG, 8)
                    tk = "SIG%s" % "ab"[hf]
                    sg = SIG[:, hf * 512:(hf + 1) * 512]
                    act(sg, ps, AF.Exp, [key], [tk], scale=-1.0)
                    act(sg, sg, AF.Ln, [tk], [tk], bias=1.0)
                    act(sg, sg, AF.Exp, [tk], [tk], scale=-1.0)
                cp("pool", PB, PF[:], ["PF"], ["PB"])
                dmaT(PTT.rearrange("p (c t) -> p c t", c=2), PB.rearrange("p (c t) -> p c t", c=2), "d_tp", ["PB"], ["PTT"])
                PTT3 = PTT.rearrange("p (c t) -> p c t", c=2)
                for hf in range(2):
                    ps, key = proj(hf * 512, 512, PTT3, "PTT", WPP, 2)
                    tk = "SIG%s" % "ab"[hf]
                    tt("dve", SIG[:, hf * 512:(hf + 1) * 512], SIG[:, hf * 512:(hf + 1) * 512], ps, ALU.mult,
                       [tk, key], [tk])
                tt("dve", H, H, SIG, ALU.add, ["H", "SIG"], ["H"])
                act(OUT, H, AF.Square, ["H"], ["OUT", "ss2"], accum=ST[:, 4:5])
                rstd_from_ss(ST[:, 4:5], ST[:, 5:6], D, "ss2", "rs2")
                stt("dve", OUT, H, ST[:, 5:6], FN[:], ALU.mult, ALU.mult, ["H", "rs2", "FN"], ["OUT"])
                dma(out_d[t * 128:(t + 1) * 128, :], OUT, "d_o", reads=["OUT"], writes=["out"])

            if LVL >= 5:
                use_ut(1)
                kv_attn_proj(0, 1)

                def front(t):
                    use_ut(t % 2)
                    kb = t % 2
                    kv_attn_proj(t + 1, kb)
                    Qb, kQ = XNs[t % 2], "XN%d" % (t % 2)
                    for hf in range(2):
                        ps, key = proj(C_AQ + hf * 512, 512, cur["UT"], cur["kUT"], WIN, 8)
                        rope_evac(ps, key, 8, Qb[:, hf * 512:(hf + 1) * 512].rearrange("p (h d) -> p h d", h=8), kQ,
                                  t + 1, False)
                    dmaT(QT.rearrange("p (c t) -> p c t", c=8), Qb[:].rearrange("p (c t) -> p c t", c=8), "d_tq", [kQ], ["QT"])
                    silu_evac(C_ZA, SA[:], "SA")
                    if t + 1 < P2T:
                        stage_A(x_d[(t + 1) * 128:(t + 2) * 128, :], (t + 1) % 2)
                        if t + 2 < P2T:
                            load_x(x_d[(t + 2) * 128:(t + 3) * 128, :])

                def back(t):
                    use_ut(t % 2)
                    kb = t % 2
                    rows = slice(t * 128, (t + 1) * 128)
                    dma(E1, sc_e1[rows, :], "d_l1", writes=["E1"])
                    dmaT(KIT.rearrange("p (c t) -> p c t", c=4), sc_ki[rows, :].rearrange("p (c t) -> p c t", c=4),
                         "d_l2", [], ["KIT"])
                    dma(KTL, sc_ki[rows, :], "d_l3", writes=["KTL"])
                    dma(VG[:].rearrange("p h d -> p (h d)"), sc_vg[rows, :], "d_l4", writes=["VG"])
                    attention(t, kb)
                    silu_evac(C_ZG, SG[:], "SG")
                    ps, key = proj(C_GQ, 512, cur["UT"], cur["kUT"], WIN, 8)
                    stt("dve", QD, ps, float(128 ** -0.5), E1, ALU.mult, ALU.mult, [key, "E1"], ["QD"])
                    dmaT(QDT.rearrange("p (c t) -> p c t", c=4), QD.rearrange("p (c t) -> p c t", c=4), "d_tqd", ["QD"], ["QDT"])
                    gla(t)

                front(0)
                for t in range(P2T):
                    S.parity = t % 2
                    back(t)
                    if t + 1 < P2T:
                        front(t + 1)
                    stage_F(t, 0)

        try:
            _phases()
        except _Stop:
            pass

        S.barrier()

        return _emit(nc, S, es)


_NC_CACHE = {}


def _get_nc():
    if "nc" not in _NC_CACHE:
        _NC_CACHE["nc"] = build_nc()
    return _NC_CACHE["nc"]


def kernel(x, p, positions, norm_mix, w_in, attn_sinks, w_gate_up, b_gate, gla_norm,
           w_out, ple_norm, w_ple_gate, w_ple_proj, final_norm):
    f32 = np.float32
    x = np.asarray(x, f32)
    p = np.asarray(p, f32)
    positions = np.asarray(positions, np.int32)
    w_in0 = np.ascontiguousarray(np.asarray(w_in, f32)[0])
    w_out0 = np.ascontiguousarray(np.asarray(w_out, f32)[0])
    w_pg0 = np.ascontiguousarray(np.asarray(w_ple_gate, f32)[0])
    w_pp0 = np.ascontiguousarray(np.asarray(w_ple_proj, f32)[0])
    nm = np.ascontiguousarray(np.asarray(norm_mix, f32)[0].reshape(8, 128).T)
    pn = np.ascontiguousarray(np.asarray(ple_norm, f32)[0].reshape(8, 128).T)
    gn = np.ascontiguousarray(np.asarray(gla_norm, f32)[0].reshape(2, 128).T)
    fn = np.ascontiguousarray(np.broadcast_to(np.asarray(final_norm, f32)[None, :], (128, D)))
    sinks = np.ascontiguousarray(np.broadcast_to(np.asarray(attn_sinks, f32)[0][None, :], (128, 16)))
    wgu = np.ascontiguousarray(np.concatenate([np.asarray(w_gate_up, f32)[0], np.asarray(b_gate, f32)[0][None, :]], 0))
    ident = np.eye(128, dtype=f32)
    ii = np.arange(128)
    tri = (ii[:, None] <= ii[None, :]).astype(f32)
    tric = np.ascontiguousarray(1.0 - tri)
    invf = np.power(f32(500000.0), -(np.arange(0, 16, 2, dtype=f32) / f32(16))).astype(f32)
    invf = np.ascontiguousarray(np.broadcast_to(invf[None, :], (128, 8)))

    in_maps = []
    for c in range(NCORES):
        b, j = c // 4, c % 4
        t0 = j * TOK
        xs = np.ascontiguousarray(x[b, t0:t0 + TOK])
        if j == 0:
            xh = np.zeros((128, D), f32)
            ph = np.zeros((128,), np.int32)
        else:
            xh = np.ascontiguousarray(x[b, t0 - 128:t0])
            ph = positions[b, t0 - 128:t0]
        pos = np.concatenate([ph, positions[b, t0:t0 + TOK]]).reshape(17, 128).T
        flags = np.zeros((128, 3), f32)
        flags[:, :j] = 1.0
        in_maps.append({
            "x": xs, "xh": xh, "p": np.ascontiguousarray(p[0, b, t0:t0 + TOK]),
            "pos": np.ascontiguousarray(pos.astype(np.int32)),
            "w_in": w_in0, "w_out": w_out0, "w_pg": w_pg0, "w_pp": w_pp0,
            "nm": nm, "pn": pn, "gn": gn, "fn": fn, "sinks": sinks, "wgu": wgu,
            "ident": ident, "tri": tri, "tric": tric,
            "mprev0": tric if j > 0 else np.zeros((128, 128), f32),
            "flags": flags, "invf": invf,
        })
    nc = _get_nc()
    res = run_bass_kernel_spmd(nc, in_maps, core_ids=list(range(NCORES)))
    out = np.empty((2, 8192, D), f32)
    for c in range(NCORES):
        b, j = c // 4, c % 4
        out[b, j * TOK:(j + 1) * TOK] = np.asarray(res.results[c]["out"], f32)
    return out
```

```python
import numpy as np
from contextlib import ExitStack
import concourse.bass as bass
import concourse.mybir as mybir
from concourse.bass_utils import run_bass_kernel_spmd

F32 = mybir.dt.float32
BF16 = mybir.dt.bfloat16
I32 = mybir.dt.int32
AF = mybir.ActivationFunctionType
ALU = mybir.AluOpType

NCORES = 8
TOK = 2048
NT = 16
D = 1024
DIN = 5392
EPS = 1e-6
PI = float(np.float32(np.pi))
TWO_PI = float(np.float32(2 * np.pi))
CW1 = 6.28125
CW2 = float(np.float32(round((2 * np.pi - CW1) * 2 ** 20) / 2 ** 20))
CW3 = float(np.float32(2 * np.pi - CW1 - CW2))

C_AQ, C_AK, C_AV, C_GQ, C_GK, C_GV, C_GL, C_ZA, C_ZG = 0, 1024, 1152, 1280, 1792, 2304, 3328, 3344, 4368


class Sched:
    ENG = ("pe", "act", "dve", "pool", "sp")
    LIST_SCHEDULE = True

    def __init__(self):
        self.items = []
        self.alias = {}
        self.open_pe = None
        self.parity = 0
        self.prog = {e: [] for e in self.ENG}
        self.cnt = {e: 0 for e in self.ENG}
        self.seen = {e: {} for e in self.ENG}
        self.dma_cnt = {}

    def keys(self, names):
        out = []
        for n in names:
            if _DBG.get("noalias") and n in _DBG["noalias"]:
                ks = (n,)
            else:
                ks = self.alias.get(n, (n,))
            if _DBG.get("dbl") and n in _DBG["dbl"]:
                ks = tuple((k, self.parity) for k in ks)
            out.extend(ks)
        return out

    def op(self, eng, fn, reads=(), writes=(), inc=True, dma=None, cost=0.3, dma_inc=16):
        reads = self.keys(reads)
        writes = self.keys(writes)
        if eng == "pe":
            if self.open_pe is None:
                g = dict(eng="pe", fns=[], reads=[], writes=[], dma=None, cost=0.0)
                self.items.append(g)
                self.open_pe = g
            g = self.open_pe
            g["fns"].append(fn)
            g["reads"].extend(reads)
            g["writes"].extend(writes)
            g["cost"] += cost
            if inc:
                self.open_pe = None
            return
        assert self.open_pe is None, "non-PE op recorded inside an open PE group"
        self.items.append(dict(eng=eng, fns=[fn], reads=list(reads), writes=list(writes), dma=dma, cost=cost,
                               dma_inc=dma_inc))

    def barrier(self):
        assert self.open_pe is None
        self.items.append(("barrier",))

    @staticmethod
    def _is_bank(k):
        return isinstance(k, str) and len(k) == 2 and k[0] == "B" and k[1].isdigit()

    def _hazards(self, ops):
        lastw, readers = {}, {}
        for idx, o in enumerate(ops):
            preds = set()
            for r in o["reads"]:
                if r in lastw:
                    preds.add(lastw[r])
                if self._is_bank(r):
                    preds.update(readers.get(r, ()))
            for w in o["writes"]:
                if w in lastw:
                    preds.add(lastw[w])
                preds.update(readers.get(w, ()))
            preds.discard(idx)
            o["preds"] = preds
            for w in o["writes"]:
                lastw[w] = idx
                readers[w] = set()
            for r in o["reads"]:
                readers.setdefault(r, set()).add(idx)

    def _order(self, ops):
        n = len(ops)
        if not self.LIST_SCHEDULE:
            return list(range(n))
        LAT = 0.15
        succs = [[] for _ in range(n)]
        npred = [0] * n
        for i, o in enumerate(ops):
            npred[i] = len(o["preds"])
            for p in o["preds"]:
                succs[p].append(i)
        bl = [0.0] * n
        for i in range(n - 1, -1, -1):
            m = 0.0
            for sidx in succs[i]:
                if bl[sidx] + LAT > m:
                    m = bl[sidx] + LAT
            bl[i] = ops[i]["cost"] + m
        ready_t = [0.0] * n
        finish = [0.0] * n
        free = {e: 0.0 for e in self.ENG}
        ready = {e: [] for e in self.ENG}
        for i in range(n):
            if npred[i] == 0:
                ready[ops[i]["eng"]].append(i)
        picks = []
        done = 0
        mode = _DBG.get("sched", "bl")
        while done < n:
            best = None
            for e in self.ENG:
                lst = ready[e]
                if not lst:
                    continue
                fe = free[e]
                bi, bkey = None, None
                for i in lst:
                    t = ready_t[i] if ready_t[i] > fe else fe
                    if mode == "bl":
                        key = (t - fe if t - fe > 0.08 else 0.0, -bl[i], i)
                    else:
                        key = (t, i)
                    if bkey is None or key < bkey:
                        bi, bkey = i, key
                t = ready_t[bi] if ready_t[bi] > fe else fe
                if best is None or (t, bi) < (best[1], best[0]):
                    best = (bi, t, e)
            i, t, e = best
            ready[e].remove(i)
            o = ops[i]
            if o["dma"]:
                free[e] = t + 0.1
                finish[i] = t + o["cost"]
            else:
                free[e] = t + o["cost"]
                finish[i] = free[e]
            picks.append((t, i))
            done += 1
            for sidx in succs[i]:
                if finish[i] + LAT > ready_t[sidx]:
                    ready_t[sidx] = finish[i] + LAT
                npred[sidx] -= 1
                if npred[sidx] == 0:
                    ready[ops[sidx]["eng"]].append(sidx)
        self.est_time = getattr(self, "est_time", 0.0) + (max(finish) if n else 0.0)
        return [i for (_, i) in picks]

    def _emit_segment(self, ops):
        self._hazards(ops)
        order = self._order(ops)
        know = {}
        pos = {i: n for n, i in enumerate(order)}
        for i in order:
            o = ops[i]
            eng = o["eng"]
            own = None if o["dma"] else "S_" + eng
            sd = self.seen[eng]
            for p in sorted(o["preds"], key=lambda q: -pos[q]):
                sem, val = ops[p]["sv"]
                if sem == own and eng == "pe":
                    continue
                if sd.get(sem, 0) >= val:
                    continue
                self.prog[eng].append(("wait", sem, val))
                for s2, v2 in know[p].items():
                    if sd.get(s2, 0) < v2:
                        sd[s2] = v2
            if o["dma"]:
                di = o.get("dma_inc", 16)
                self.dma_cnt[o["dma"]] = self.dma_cnt.get(o["dma"], 0) + di
                sem, val, incn = o["dma"], self.dma_cnt[o["dma"]], di
            else:
                self.cnt[eng] += 1
                sem, val, incn = own, self.cnt[eng], 1
            o["sv"] = (sem, val)
            kn = dict(sd)
            kn[sem] = val
            know[i] = kn
            fns = o["fns"]
            for k, fn in enumerate(fns):
                self.prog[eng].append(("op", fn, sem, incn if k == len(fns) - 1 else 0))

    def _emit_barrier(self):
        tot = {("S_" + e): self.cnt[e] for e in ("pe", "act", "dve", "pool")}
        tot.update(self.dma_cnt)
        for e in self.ENG:
            for sem, val in tot.items():
                if val > 0 and sem != "S_" + e and self.seen[e].get(sem, 0) < val:
                    self.seen[e][sem] = val
                    self.prog[e].append(("wait", sem, val))

    def finalize(self):
        assert self.open_pe is None
        seg = []
        for it in self.items + [("barrier",)]:
            if isinstance(it, tuple):
                if seg:
                    self._emit_segment(seg)
                    seg = []
                self._emit_barrier()
            else:
                seg.append(it)


def _emit(nc, S, es):
    S.finalize()
    sem_names = ["S_pe", "S_act", "S_dve", "S_pool"] + sorted(S.dma_cnt)
    sems = {n: es.enter_context(nc.semaphore(n)) for n in sem_names}
    block = es.enter_context(nc.Block())

    def run(eng_name):
        def f(e):
            for it in S.prog[eng_name]:
                if it[0] == "wait":
                    e.wait_ge(sems[it[1]], it[2])
                else:
                    _, fn, sem, incn = it
                    ins = fn(e)
                    if incn:
                        ins.then_inc(sems[sem], incn)
        return f

    block.sync(run("sp"))
    block.tensor(run("pe"))
    block.scalar(run("act"))
    block.vector(run("dve"))
    block.gpsimd(run("pool"))
    return nc


_DBG = {"level": 9, "tiles": NT, "p1tiles": NT, "stop": None}


class _Stop(Exception):
    pass


def ck(n):
    if _DBG["stop"] == n:
        raise _Stop()


def build_nc():
    nc = bass.Bass("TRN2", target_bir_lowering=False)
    LVL = _DBG["level"]

    def din(name, shape, dt=F32):
        return nc.dram_tensor(name, list(shape), dt, kind="ExternalInput").ap()

    x_d = din("x", [TOK, D])
    xh_d = din("xh", [128, D])
    p_d = din("p", [TOK, 256])
    pos_d = din("pos", [128, 17], I32)
    win_d = din("w_in", [D, DIN])
    wout_d = din("w_out", [2048, D])
    wpg_d = din("w_pg", [D, D])
    wpp_d = din("w_pp", [256, D])
    nm_d = din("nm", [128, 8])
    pn_d = din("pn", [128, 8])
    gn_d = din("gn", [128, 2])
    fn_d = din("fn", [128, D])
    sinks_d = din("sinks", [128, 16])
    wgu_d = din("wgu", [17, 512])
    ident_d = din("ident", [128, 128])
    tri_d = din("tri", [128, 128])
    tric_d = din("tric", [128, 128])
    mprev0_d = din("mprev0", [128, 128])
    flags_d = din("flags", [128, 3])
    invf_d = din("invf", [128, 8])
    out_d = nc.dram_tensor("out", [TOK, D], F32, kind="ExternalOutput").ap()
    cc_in = nc.dram_tensor("cc_in", [128, 1040], F32)
    cc_out = nc.dram_tensor("cc_out", [512, 1040], F32)
    sc_e1 = nc.dram_tensor("sc_e1", [NT * 128, 512], F32).ap()
    sc_ki = nc.dram_tensor("sc_ki", [NT * 128, 512], BF16).ap()
    sc_ktl = nc.dram_tensor("sc_ktl", [NT * 128, 512], BF16).ap()
    sc_vg = nc.dram_tensor("sc_vg", [NT * 128, 1024], BF16).ap()

    S = Sched()
    es = ExitStack()
    with es:
        def sb(name, shape, dt):
            return es.enter_context(nc.sbuf_tensor(name, list(shape), dt))

        def psb(name, shape, dt):
            return es.enter_context(nc.psum_tensor(name, list(shape), dt))

        WIN = sb("WIN", [128, 8, DIN], BF16)
        WOUT = sb("WOUT", [128, 16, D], BF16)
        WPG = sb("WPG", [128, 8, D], BF16)
        WPP = sb("WPP", [128, 2, D], BF16)
        IDENT = sb("IDENT", [128, 128], F32)
        IDB = sb("IDB", [128, 128], BF16)
        TRI = sb("TRI", [128, 128], F32)
        TRIC = sb("TRIC", [128, 128], F32)
        MASK2 = sb("MASK2", [128, 2, 128], BF16)
        MASK0 = sb("MASK0", [128, 2, 128], BF16)
        FN = sb("FN", [128, D], F32)
        CC = sb("CC", [128, 17, 16], F32)
        SS = sb("SS", [128, 17, 16], F32)
        WGU = sb("WGU", [32, 512], F32)
        GLT = sb("GLT", [32, 128], F32)
        ESINK = sb("ESINK", [128, 16], F32)
        NM = sb("NM", [128, 8], F32)
        PN = sb("PN", [128, 8], F32)
        GN = sb("GN", [128, 2], F32)
        FLAGS = sb("FLAGS", [128, 3], F32)
        ONES = sb("ONES", [128, 2], F32)
        ST = sb("ST", [128, 32], F32)
        DLOG = sb("DLOG", [128, 4], F32)
        DEC = sb("DEC", [128, 4], F32)
        X = [sb("X0", [128, D], F32)]
        XN = sb("XN", [128, D], BF16)
        UT = sb("UT", [128, 8, 128], BF16)
        XNs = [XN, sb("XNb", [128, D], BF16)]
        UTs = [UT, sb("UTb", [128, 8, 128], BF16)]
        cur = {"UT": UT, "kUT": "UT"}
        K2 = sb("K2", [128, 2, 2, 64], BF16)
        KTD = [sb("KTD0", [128, 2, 128], BF16), sb("KTD1", [128, 2, 128], BF16)]
        VA = [sb("VA0", [128, 2, 65], BF16), sb("VA1", [128, 2, 65], BF16)]
        VG = sb("VG", [128, 4, 256], BF16)
        SA = sb("SA", [128, D], BF16)
        SG = sb("SG", [128, D], BF16)
        PT = [sb("PT%d" % i, [128, 2, 128], BF16) for i in range(4)]
        Y = sb("Y", [128, 2048], BF16)
        SST = sb("SST", [128, 4, 256], F32)
        SB = sb("SB", [128, 4, 256], BF16)
        PF = sb("PF", [128, 256], F32)
        GL = sb("GL", [128, 16], F32)
        RT = [sb("RT0", [128, 128], F32), sb("RT1", [128, 128], F32)]
        GL1 = PF[:, 128:144]
        GLT1 = PF[0:32, 0:128]
        DEC1 = PF[:, 144:148]
        DECALL = sb("DECALL", [128, NT, 4], F32)
        R1 = sb("R1", [128, 6400], F32)

        def r1(off, nbytes, dt):
            v = R1[:, off // 4:(off + nbytes) // 4]
            return v if dt == F32 else v.bitcast(dt)

        def r1keys(off, nbytes):
            return tuple(("R1", g) for g in range(off // 1024, (off + nbytes + 1023) // 1024))

        def r1buf(name, off, nbytes, dt):
            S.alias[name] = r1keys(off, nbytes)
            return r1(off, nbytes, dt)

        QT = r1buf("QT", 0, 2048, BF16)
        TS = [r1buf("TS0", 2048, 2048, F32), r1buf("TS1", 4096, 2048, F32)]
        E1 = r1buf("E1", 6144, 2048, F32)
        QD = r1buf("QD", 8192, 1024, BF16)
        QDT = r1buf("QDT", 9216, 1024, BF16)
        KIT = r1buf("KIT", 10240, 1024, BF16)
        KTL = r1buf("KTL", 11264, 1024, BF16)
        OG = r1buf("OG", 16384, 4096, F32)
        YT = r1buf("YT", 12288, 4096, BF16)
        HN = r1buf("HN", 12288, 2048, BF16)
        HNT = r1buf("HNT", 14336, 2048, BF16)
        H = r1buf("H", 16384, 4096, F32)
        SIG = r1buf("SIG", 20480, 4096, F32)
        OUT = r1buf("OUT", 20480, 4096, F32)
        PB = r1buf("PB", 24576, 512, BF16)
        PTT = r1buf("PTT", 25088, 512, BF16)
        S.alias["SIGa"] = r1keys(20480, 2048)
        S.alias["SIGb"] = r1keys(22528, 2048)
        P1LG = r1buf("P1LG", 4096, 2048, F32)
        P1E3 = r1buf("P1E3", 10240, 2048, F32)
        P1KTL = r1buf("P1KTL", 14336, 1024, BF16)
        STG = [r1buf("STG%d" % i, i * 6208, 6208, F32) for i in range(4)]
        GX = r1buf("GX", 0, 3 * 1040 * 4, F32)
        Yf = Y[:].bitcast(F32)
        SAf = SA[:].bitcast(F32)
        ROPE = [Yf[:, i * 136:(i + 1) * 136] for i in range(7)] + [SAf[:, 0:136]]
        ROPEI = SAf[:, 136:272].bitcast(I32)
        POSI = SAf[:, 272:289].bitcast(I32)
        MP0 = SAf[:, 384:512]
        LGb = r1buf("LGb", 0, 2048, F32)
        E3b = r1buf("E3b", 2048, 2048, F32)
        KTLb = r1buf("KTLb", 6144, 1024, BF16)
        E2b = r1buf("E2b", 7168, 2048, F32)
        KIa = r1buf("KIa", 12288, 1024, BF16)
        KIb = r1buf("KIb", 13312, 1024, BF16)
        CCS = r1buf("CCS", 0, 4160, F32)
        S.alias["INVF"] = ("ROPE0",)
        for a_, b_ in (("XN0", "XN"), ("XN1", "XNb"), ("UTp0", "UT"), ("UTp1", "UTb"),
                       ("ss0_0", "p1ss0"), ("rs0_0", "p1rs0"), ("ss0_1", "p1ss1"), ("rs0_1", "p1rs1")):
            S.alias[a_] = (b_,)
        S.alias["YTa"] = r1keys(12288, 2048)
        S.alias["YTb"] = r1keys(14336, 2048)
        for nm_, bk in (("B2a", "B2"), ("B2b", "B2"), ("B3a", "B3"), ("B3b", "B3"), ("B3c", "B3"), ("B3d", "B3"),
                        ("B4L", "B4"), ("B4R", "B4"), ("B5L", "B5"), ("B5R", "B5"), ("B6L", "B6"), ("B6R", "B6")):
            S.alias[nm_] = (bk,)
        if _DBG.get("vtp"):
            S.alias["B2a"] = ("VTa",)
            S.alias["B2b"] = ("VTb",)
        S.alias["ssg"] = ("ssg0", "ssg1", "ssg2", "ssg3")
        for h in range(4):
            S.alias["OG%d" % h] = r1keys(16384 + h * 1024, 1024)
            S.alias["OGx%d" % h] = r1keys(16384 + h * 1024, 1024)
        S.alias["Yall"] = ("Ya0", "Ya1", "Ya2", "Ya3", "Yg0", "Yg1", "Yg2", "Yg3")

        BK = [psb("B%d" % i, [128, 512], F32) if i != 2 else None for i in range(8)]
        B2bf = psb("B2", [128, 1024], BF16)[:]
        B2f32 = B2bf.bitcast(F32)

        def fsz(ap):
            n = 1
            for d in ap.shape[1:]:
                n *= d
            return n

        def ecost(eng, ap):
            n = fsz(ap)
            if eng == "act":
                return n / 1200.0 + 0.22
            if eng == "dve":
                return n / 960.0 + 0.12
            return n / 500.0 + 0.3

        def dma(out, in_, sem, reads=(), writes=(), eng="sp"):
            nb = fsz(out) * out.shape[0] * 4
            return S.op(eng, lambda e, o=out, i=in_: e.dma_start(out=o, in_=i), reads, writes, dma=sem,
                        cost=2.0 + nb / 150e3)

        def dmaT(out, in_, sem, reads=(), writes=()):
            nb = fsz(out) * out.shape[0] * 2
            return S.op("sp", lambda e, o=out, i=in_: e.dma_start_transpose(out=o, in_=i), reads, writes, dma=sem,
                        cost=2.5 + nb / 150e3)

        def mm(out, lhsT, rhs, start, stop, reads, writes, inc=None):
            inc = stop if inc is None else inc
            c = fsz(rhs) * (4 if lhsT.dtype == F32 else 1) / _DBG.get("pe_rate", 1800.0) + _DBG.get("pe_fix", 0.12)
            return S.op("pe", lambda e, o=out, l=lhsT, r=rhs, s=start, t=stop:
                        e.matmul(out=o, lhsT=l, rhs=r, start=s, stop=t), reads, writes, inc=inc, cost=c)

        def tp(out, in_, ident, reads, writes, inc=True):
            return S.op("pe", lambda e, o=out, i=in_, d=ident: e.transpose(out=o, in_=i, identity=d),
                        reads, writes, inc=inc, cost=0.1)

        def act(out, in_, func, reads, writes, scale=None, bias=None, accum=None):
            kw = {}
            if scale is not None:
                kw["scale"] = scale
            if bias is not None:
                kw["bias"] = bias
            if accum is not None:
                kw["accum_out"] = accum
            return S.op("act", lambda e, o=out, i=in_, f=func, k=kw: e.activation(out=o, in_=i, func=f, **k),
                        reads, writes, cost=ecost("act", out))

        def tt(eng, out, in0, in1, op, reads, writes):
            return S.op(eng, lambda e, o=out, a=in0, b=in1, p=op: e.tensor_tensor(out=o, in0=a, in1=b, op=p),
                        reads, writes, cost=ecost(eng, out))

        def tsc(eng, out, in0, s1, op0, reads, writes, s2=None, op1=None):
            if op1 is None:
                return S.op(eng, lambda e, o=out, a=in0, s=s1, p=op0:
                            e.tensor_scalar(out=o, in0=a, scalar1=s, scalar2=None, op0=p), reads, writes,
                            cost=ecost(eng, out))
            return S.op(eng, lambda e, o=out, a=in0, s=s1, p=op0, s_2=s2, p1=op1:
                        e.tensor_scalar(out=o, in0=a, scalar1=s, scalar2=s_2, op0=p, op1=p1), reads, writes,
                        cost=ecost(eng, out))

        def stt(eng, out, in0, scalar, in1, op0, op1, reads, writes):
            return S.op(eng, lambda e, o=out, a=in0, s=scalar, b=in1, p0=op0, p1=op1:
                        e.scalar_tensor_tensor(out=o, in0=a, scalar=s, in1=b, op0=p0, op1=p1), reads, writes,
                        cost=ecost(eng, out))

        def cp(eng, out, in_, reads, writes):
            if eng == "act":
                return act(out, in_, AF.Copy, reads, writes)
            return S.op(eng, lambda e, o=out, i=in_: e.tensor_copy(out=o, in_=i), reads, writes,
                        cost=ecost(eng, out))

        def memset(eng, ap, val, writes):
            return S.op(eng, lambda e, a=ap, v=val: e.memset(a, v), (), writes, cost=ecost(eng, ap))

        def rstd_from_ss(ss_ap, out_ap, n, key_ss, key_out):
            act(out_ap, ss_ap, AF.Ln, [key_ss], [key_out], scale=1.0 / n, bias=EPS)
            act(out_ap, out_ap, AF.Exp, [key_out], [key_out], scale=-0.5)

        dma(IDENT[:], ident_d, "d_c0", writes=["IDENT"])
        dma(TRI[:], tri_d, "d_c1", writes=["TRI"])
        dma(TRIC[:], tric_d, "d_c2", writes=["TRIC"])
        dma(MP0, mprev0_d, "d_c3", writes=["MP0"])
        dma(ESINK[:], sinks_d, "d_c5", writes=["ESINK"])
        dma(NM[:], nm_d, "d_c6", writes=["NM"])
        dma(PN[:], pn_d, "d_c7", writes=["PN"])
        dma(GN[:], gn_d, "d_c8", writes=["GN"])
        dma(FLAGS[:], flags_d, "d_c9", writes=["FLAGS"])
        dma(WGU[0:17, :], wgu_d, "d_c10", writes=["WGU"])
        dma(POSI, pos_d, "d_c11", writes=["POSI"])
        dma(ROPE[0][:, 0:8], invf_d, "d_c12", writes=["INVF"])
        if LVL < 0.15:
            S.barrier()
            return _emit(nc, S, es)
        cp("dve", IDB[:], IDENT[:], ["IDENT"], ["IDB"])
        cp("dve", MASK2[:, 0, :], TRIC[:], ["TRIC"], ["MASK2"])
        cp("dve", MASK2[:, 1, :], TRI[:], ["TRI"], ["MASK2"])
        cp("dve", MASK0[:, 0, :], MP0, ["MP0"], ["MASK0"])
        cp("dve", MASK0[:, 1, :], TRI[:], ["TRI"], ["MASK0"])
        memset("dve", ONES[:], 1.0, ["ONES"])
        memset("dve", GLT[:], 1.0, ["GLT"])
        memset("dve", GLT1, 1.0, ["GLT1"])
        memset("dve", DLOG[:], 0.0, ["DLOG"])
        memset("dve", SST[:], 0.0, ["SST"])
        memset("dve", SB[:], 0.0, ["SB"])
        for b in range(2):
            memset("dve", VA[b][:], 1.0, ["VA%d" % b])
            memset("dve", KTD[b][:], 0.0, ["KTD%d" % b])
        act(ESINK[:], ESINK[:], AF.Exp, ["ESINK"], ["ESINK"])

        if LVL < 0.25:
            S.barrier()
            return _emit(nc, S, es)
        INVF = ROPE[0][:, 0:8]
        POSF = ROPE[1][:, 0:17]
        ANG = ROPE[2][:, 0:136].rearrange("p (t f) -> p t f", f=8)
        KF = ROPE[3][:, 0:136].rearrange("p (t f) -> p t f", f=8)
        RR = ROPE[4][:, 0:136].rearrange("p (t f) -> p t f", f=8)
        R2 = ROPE[5][:, 0:136].rearrange("p (t f) -> p t f", f=8)
        MM_ = ROPE[6][:, 0:136].rearrange("p (t f) -> p t f", f=8)
        SINV = ROPE[7][:, 0:136].rearrange("p (t f) -> p t f", f=8)
        KI32 = ROPEI[:, 0:136].rearrange("p (t f) -> p t f", f=8)
        cp("dve", POSF, POSI, ["POSI"], ["ROPE1"])
        tt("dve", ANG, POSF.unsqueeze(2).broadcast_to([128, 17, 8]),
           INVF.unsqueeze(1).broadcast_to([128, 17, 8]), ALU.mult, ["ROPE1", "ROPE0"], ["ROPE2"])
        tsc("dve", KI32, ANG, 1.0 / (2 * np.pi), ALU.mult, ["ROPE2"], ["ROPEI"])
        cp("dve", KF, KI32, ["ROPEI"], ["ROPE3"])
        stt("dve", RR, KF, -CW1, ANG, ALU.mult, ALU.add, ["ROPE3", "ROPE2"], ["ROPE4"])
        stt("dve", RR, KF, -CW2, RR, ALU.mult, ALU.add, ["ROPE3", "ROPE4"], ["ROPE4"])
        stt("dve", RR, KF, -CW3, RR, ALU.mult, ALU.add, ["ROPE3", "ROPE4"], ["ROPE4"])
        tsc("dve", RR, RR, PI, ALU.min, ["ROPE4"], ["ROPE4"], s2=-PI, op1=ALU.max)
        act(SINV, RR, AF.Sin, ["ROPE4"], ["ROPE7"])
        tsc("dve", R2, RR, float(np.pi / 2), ALU.add, ["ROPE4"], ["ROPE5"])
        tsc("dve", MM_, R2, PI, ALU.is_gt, ["ROPE5"], ["ROPE6"], s2=TWO_PI, op1=ALU.mult)
        tt("dve", R2, R2, MM_, ALU.subtract, ["ROPE5", "ROPE6"], ["ROPE5"])
        tsc("dve", R2, R2, PI, ALU.min, ["ROPE5"], ["ROPE5"], s2=-PI, op1=ALU.max)
        act(CC[:, :, 0:8], R2, AF.Sin, ["ROPE5"], ["CC"])
        cp("dve", CC[:, :, 8:16], CC[:, :, 0:8], ["CC"], ["CC"])
        cp("dve", SS[:, :, 8:16], SINV, ["ROPE7"], ["SS"])
        tsc("dve", SS[:, :, 0:8], SINV, -1.0, ALU.mult, ["ROPE7"], ["SS"])

        STG2 = [r1buf("STGa", 15360, 5120, F32), r1buf("STGb", 20480, 5120, F32)]
        wstate = {"i": 0}

        def wpiece(src, n, dst, sc, slots, tag, engs):
            i = wstate["i"]
            wstate["i"] += 1
            sl = i % len(slots)
            dma(slots[sl][:, 0:n], src, "d_%s%d" % (tag, sl), writes=["%s%d" % (tag, sl)])
            eng = engs[i % len(engs)]
            rd = ["%s%d" % (tag, sl), "NM", "PN", "GN"]
            wk = [("Wp", i)]
            if sc is None:
                cp(eng, dst, slots[sl][:, 0:n], rd, wk)
            elif eng == "act":
                act(dst, slots[sl][:, 0:n], AF.Copy, rd, wk, scale=sc)
            else:
                tsc(eng, dst, slots[sl][:, 0:n], sc, ALU.mult, rd, wk)

        S.alias["STGa0"] = S.alias["STGa"]
        S.alias["STGa1"] = S.alias["STGb"]
        if LVL >= 2:
            for c in range(8):
                wpiece(win_d[c * 128:(c + 1) * 128, C_GK:C_ZA], C_ZA - C_GK, WIN[:, c, C_GK:C_ZA], NM[:, c:c + 1],
                       STG, "STG", ["dve", "act"])
        pieces2 = []
        for c in range(8):
            for (c0, c1) in ((0, 1280), (1280, C_GK), (C_ZA, C_ZA + 1280), (C_ZA + 1280, DIN)):
                pieces2.append((win_d[c * 128:(c + 1) * 128, c0:c1], c1 - c0, WIN[:, c, c0:c1], NM[:, c:c + 1]))
        for c in range(16):
            sc = None if c < 8 else GN[:, (c % 2):(c % 2) + 1]
            pieces2.append((wout_d[c * 128:(c + 1) * 128, :], 1024, WOUT[:, c, :], sc))
        for c in range(8):
            pieces2.append((wpg_d[c * 128:(c + 1) * 128, :], 1024, WPG[:, c, :], PN[:, c:c + 1]))
        for c in range(2):
            pieces2.append((wpp_d[c * 128:(c + 1) * 128, :], 1024, WPP[:, c, :], None))
        if LVL < 2:
            pieces2 = []

        def stream_weights(k):
            for _ in range(k):
                if pieces2:
                    src, n, dst, sc = pieces2.pop(0)
                    wpiece(src, n, dst, sc, STG2, "STGa", ["act", "act", "dve"])
        S.barrier()

        def load_x(xsrc, xb=0):
            dma(X[0][:], xsrc, "d_x0", writes=["X0"])

        def stage_A(xsrc, par, xbuf=None, kx="X0"):
            xb_ = X[0][:] if xbuf is None else xbuf
            XNp, UTp = XNs[par], UTs[par]
            kXN, kUT = "XN%d" % par, "UTp%d" % par
            so = 0 if par == 0 else 6
            kss, krs = "ss0_%d" % par, "rs0_%d" % par
            act(XNp[:], xb_, AF.Square, [kx], [kXN, kss], accum=ST[:, so:so + 1])
            rstd_from_ss(ST[:, so:so + 1], ST[:, so + 1:so + 2], D, kss, krs)
            tsc("dve", XNp[:], xb_, ST[:, so + 1:so + 2], ALU.mult, [kx, krs], [kXN])
            dmaT(UTp[:], XNp[:].rearrange("p (c t) -> p c t", c=8), "d_tu%d" % par, [kXN], [kUT])

        def use_ut(par):
            cur["UT"], cur["kUT"] = UTs[par], "UTp%d" % par

        pbank = [0]

        def proj(col0, ncols, lhs, lhs_key, W, nk, wcol0=None):
            b = (0, 1, 3, 2)[pbank[0] % 4]
            pbank[0] += 1
            key = "B%d" % b
            if _DBG.get("vproj"):
                pbank.append(0)
                key = "VP%d" % (len(pbank) % _DBG["vproj"])
            ps = (BK[b][:] if b != 2 else B2f32)[:, 0:ncols]
            for k in range(nk):
                if isinstance(lhs_key, list):
                    half = nk // len(lhs_key)
                    lk = lhs_key[k // half]
                    ginc = (k % half == half - 1)
                else:
                    lk, ginc = lhs_key, (k == nk - 1)
                mm(ps, lhs[:, k, :], W[:, k, col0:col0 + ncols], k == 0, k == nk - 1,
                   [lk, "W"], [key], inc=ginc)
            return ps, key

        def gla_kv_state(h, t):
            hk = "B4%s" % "LR"[h % 2]
            ps = BK[4][:, (h % 2) * 256:(h % 2) * 256 + 256]
            mm(ps, KTL[:, h * 128:(h + 1) * 128], VG[:, h, :], True, True, ["KTL", "VG"], [hk])
            tsc("dve", SST[:, h, :], SST[:, h, :], DECALL[:, t, h:h + 1], ALU.mult, ["SST%d" % h], ["SST%d" % h])
            stt("dve", SST[:, h, :], ps, DECALL[:, t, h:h + 1], SST[:, h, :], ALU.mult, ALU.add,
                [hk, "SST%d" % h], ["SST%d" % h])

        p1bank = [0]
        P1_BANKS = [0, 1, 6, 7, 2]

        def p1_proj(col0, ncols, lhs, lhs_key):
            b = P1_BANKS[p1bank[0] % len(P1_BANKS)]
            p1bank[0] += 1
            key = "B%d" % b
            ps = (BK[b][:] if b != 2 else B2f32)[:, 0:ncols]
            for k in range(8):
                mm(ps, lhs[:, k, :], WIN[:, k, col0:col0 + ncols], k == 0, k == 7, [lhs_key, "W"], [key])
            return ps, key

        def p1_tile(t):
            par = (t + 1) % 2
            sfx = "" if par == 0 else "b"
            XNp = XNs[par][:]
            UTp = UTs[par][:]
            VGp = VG[:] if par == 0 else Y[:, 0:1024].rearrange("p (h d) -> p h d", h=4)
            E1p = (SA if par == 0 else SG)[:].bitcast(F32)
            E2p = Y[:, 1024:2048].bitcast(F32) if par == 0 else E2b
            KIp = KIa if par == 0 else KIb
            kE1, kE2, kKI = ("SA", "Yp1b", "KIa") if par == 0 else ("SG", "E2b", "KIb")
            LGp, E3p, KTLp = (P1LG, P1E3, P1KTL) if par == 0 else (LGb, E3b, KTLb)
            GLp, GLTp, DECp = (GL, GLT, DEC) if par == 0 else (GL1, GLT1, DEC1)
            kXN, kUT, kVG = ("XN", "UT", "VG") if par == 0 else ("XNb", "UTb", "Yp1")
            kLG, kE3, kKTL = ("P1LG", "P1E3", "P1KTL") if par == 0 else ("LGb", "E3b", "KTLb")
            kGL, kGLT, kDEC = "GL%d" % par, "GLT%d" % par, "DEC%d" % par
            so = 0 if par == 0 else 6
            kss, krs = "p1ss%d" % par, "p1rs%d" % par
            kx = "X0"
            act(XNp, X[0][:], AF.Square, [kx], [kXN, kss], accum=ST[:, so:so + 1])
            rstd_from_ss(ST[:, so:so + 1], ST[:, so + 1:so + 2], D, kss, krs)
            tsc("dve", XNp, X[0][:], ST[:, so + 1:so + 2], ALU.mult, [kx, krs], [kXN])
            if t + 1 < _DBG["p1tiles"]:
                load_x(x_d[(t + 1) * 128:(t + 2) * 128, :])
            dmaT(UTp, XNp.rearrange("p (c t) -> p c t", c=8), "d_tu%d" % par, [kXN], [kUT])
            for k in range(8):
                mm(BK[3][0:16, 0:128], WIN[:, k, C_GL:C_GL + 16], UTp[:, k, :], k == 0, k == 7, [kUT, "W"], ["B3"])
            cp("dve", GLTp[0:16, :], BK[3][0:16, 0:128], ["B3"], [kGLT])
            mm(BK[4][:], GLTp[0:17, :], WGU[0:17, :], True, True, [kGLT, "WGU"], ["B4"])
            act(LGp, BK[4][:], AF.Exp, ["B4"], [kLG], scale=-1.0)
            act(LGp, LGp, AF.Ln, [kLG], [kLG], bias=1.0)
            LHI = E3p.bitcast(BF16)[:, 0:512]
            LLO = E3p.bitcast(BF16)[:, 512:1024]
            cp("act", LHI, LGp, [kLG], [kE3 + "h"])
            tt("dve", LLO, LGp, LHI, ALU.subtract, [kLG, kE3 + "h"], [kE3 + "l"])
            for h in range(4):
                mm(BK[3][:, 128 + h:129 + h], LGp[:, h * 128:(h + 1) * 128], ONES[:, 0:1], True, True,
                   [kLG, "ONES"], ["B3"], inc=(h == 3))
            mm(BK[4][:], MASK2[:, 1, :], LHI, True, False, ["MASK2", kE3 + "h"], ["B4"], inc=False)
            mm(BK[4][:], MASK2[:, 1, :], LLO, False, True, ["MASK2", kE3 + "l"], ["B4"], inc=True)
            tt("dve", DLOG[:], DLOG[:], BK[3][:, 128:132], ALU.add, ["DLOG", "B3"], ["DLOG"])
            act(DECp[:], BK[3][:, 128:132], AF.Exp, ["B3"], [kDEC], scale=-1.0 / 16)
            cp("pool", DECALL[:, t, :], DECp[:], [kDEC], [("DECALL", t)])
            act(E1p, BK[4][:], AF.Exp, ["B4"], [kE1], scale=-1.0 / 16)
            act(E2p, BK[4][:], AF.Exp, ["B4"], [kE2], scale=1.0 / 16)
            rows = slice(t * 128, (t + 1) * 128)
            dma(sc_e1[rows, :], E1p, "d_s1%d" % par, reads=[kE1], writes=[("sc_e1", t)])
            ps, key = p1_proj(C_GK, 512, UTp, kUT)
            tt("dve", KIp, ps, E2p, ALU.mult, [key, kE2], [kKI])
            dma(sc_ki[rows, :], KIp, "d_s3%d" % par, reads=[kKI], writes=[("sc_ki", t)])
            for hf in range(2):
                ps, key = p1_proj(C_GV + hf * 512, 512, UTp, kUT)
                cp("act", VGp[:, hf * 2:hf * 2 + 2, :], ps.rearrange("p (h d) -> p h d", h=2), [key], [kVG])
            dma(sc_vg[rows, :], VGp.rearrange("p h d -> p (h d)"), "d_s4%d" % par, reads=[kVG], writes=[("sc_vg", t)])
            for h in range(4):
                ps = BK[5][:, (h % 2) * 256:(h % 2) * 256 + 256]
                mm(ps, KIp[:, h * 128:(h + 1) * 128], VGp[:, h, :], True, True, [kKI, kVG], ["B5"])
                tsc("dve", SST[:, h, :], SST[:, h, :], DECp[:, h:h + 1], ALU.mult, ["SST%d" % h, kDEC], ["SST%d" % h])
                stt("dve", SST[:, h, :], ps, DECp[:, h:h + 1], SST[:, h, :], ALU.mult, ALU.add,
                    ["B5", kDEC, "SST%d" % h], ["SST%d" % h])

        def _phases():
            P1T = _DBG["p1tiles"] if LVL >= 3 else 0
            if P1T:
                load_x(x_d[0:128, :], 0)
            for t in range(P1T):
                stream_weights(4)
                p1_tile(t)
            stream_weights(1000)
            P2T = _DBG["tiles"]
            if LVL >= 5:
                load_x(xh_d)
                dma(FN[:], x_d[0:128, :], "d_xf", writes=["FN"])
                stage_A(xh_d, 1)
                if P2T > 1:
                    load_x(x_d[128:256, :])
                stage_A(x_d[0:128, :], 0, xbuf=FN[:], kx="FN")
            if LVL >= 4:
                act(DLOG[:], DLOG[:], AF.Exp, ["DLOG"], ["DLOG"], scale=-1.0 / 16)
                S.barrier()
                dma(FN[:], fn_d, "d_c4", writes=["FN"])
                CCSx = r1buf("CCSx", 12288, 4160, F32)
                GXx = r1buf("GXx", 12288, 3 * 1040 * 4, F32)
                CCS3 = CCSx.rearrange("p (h c) -> p h c", h=4)
                memset("dve", CCSx, 0.0, ["CCSx"])
                cp("dve", CCS3[:, :, 0:256], SST[:], ["SST0", "SST1", "SST2", "SST3", "CCSx"], ["CCSx"])
                cp("dve", CCS3[:, :, 256], DLOG[:], ["DLOG", "CCSx"], ["CCSx"])
                dma(cc_in.ap(), CCSx, "d_cc", reads=["CCSx"], writes=["cc_in"])
                S.op("pool", lambda e: e.collective_compute(
                    "AllGather", ALU.bypass, replica_groups=[[0, 1, 2, 3], [4, 5, 6, 7]],
                    ins=[cc_in.ap().opt()], outs=[cc_out.ap().opt()]), ["cc_in"], ["cc_out"], cost=45.0,
                    dma="d_coll", dma_inc=1)
                GXv = GXx.rearrange("p (r h c) -> p r h c", r=3, h=4)
                dma(GXx.rearrange("p (r c) -> p r c", r=3), cc_out.ap()[0:384, :].rearrange("(r p) c -> p r c", p=128),
                    "d_cc2", reads=["cc_out"], writes=["GXx"])
                AR = ST[:, 8:20].rearrange("p (r h) -> p r h", r=3)
                for r in range(3):
                    tsc("dve", AR[:, r, :], GXv[:, r, :, 256], FLAGS[:, r:r + 1], ALU.mult, ["GXx", "FLAGS"], ["AR"],
                        s2=FLAGS[:, r:r + 1], op1=ALU.subtract)
                    tsc("dve", AR[:, r, :], AR[:, r, :], 1.0, ALU.add, ["AR"], ["AR"])
                for r in (1, 2):
                    tsc("dve", GXv[:, r, :, 0:256], GXv[:, r, :, 0:256], FLAGS[:, r:r + 1], ALU.mult,
                        ["GXx", "FLAGS"], ["GXx"])
                for h in range(4):
                    tsc("dve", SST[:, h, :], GXv[:, 0, h, 0:256], FLAGS[:, 0:1], ALU.mult, ["GXx", "FLAGS"], ["SST%d" % h])
                    for r in (1, 2):
                        stt("dve", SST[:, h, :], SST[:, h, :], AR[:, r, h:h + 1], GXv[:, r, h, 0:256],
                            ALU.mult, ALU.add, ["SST%d" % h, "AR", "GXx"], ["SST%d" % h])
                    cp("act", SB[:, h, :], SST[:, h, :], ["SST%d" % h], ["SB%d" % h])

            def rope_evac(ps, key, nh, dst3, dst_key, tt_idx, dup):
                src = ps.rearrange("p (h d) -> p h d", h=nh)
                cc = CC[:, tt_idx, :].unsqueeze(1).broadcast_to([128, nh, 16])
                ssl = SS[:, tt_idx, 0:8].unsqueeze(1).broadcast_to([128, nh, 8])
                ssh = SS[:, tt_idx, 8:16].unsqueeze(1).broadcast_to([128, nh, 8])
                TA = RT[0][:, 0:nh * 16].rearrange("p (h d) -> p h d", h=nh)
                TB = RT[1][:, 0:nh * 16].rearrange("p (h d) -> p h d", h=nh)
                tt("dve", TA, src[:, :, 0:16], cc, ALU.mult, [key, "CC"], ["RT0"])
                tt("dve", TB[:, :, 0:8], src[:, :, 8:16], ssl, ALU.mult, [key, "SS"], ["RT1"])
                tt("dve", TB[:, :, 8:16], src[:, :, 0:8], ssh, ALU.mult, [key, "SS"], ["RT1"])
                if dup:
                    for cpy in range(2):
                        tt("dve", dst3[:, :, cpy, 0:16], TA, TB, ALU.add, ["RT0", "RT1"], [dst_key])
                        cp("act", dst3[:, :, cpy, 16:64], src[:, :, 16:64], [key], [dst_key])
                else:
                    tt("dve", dst3[:, :, 0:16], TA, TB, ALU.add, ["RT0", "RT1"], [dst_key])
                    cp("act", dst3[:, :, 16:64], src[:, :, 16:64], [key], [dst_key])

            def kv_attn_proj(tt_idx, kb):
                ps, key = proj(C_AK, 256, cur["UT"], cur["kUT"], WIN, 8)
                rope_evac(ps[:, 0:128], key, 2, K2[:], "K2", tt_idx, True)
                cp("act", VA[kb][:, :, 0:64], ps[:, 128:256].rearrange("p (g d) -> p g d", g=2), [key], ["VA%d" % kb])
                dmaT(KTD[kb][:], K2[:].rearrange("p g c d -> p g (c d)"), "d_tk%d" % kb, ["K2"], ["KTD%d" % kb])

            def silu_evac(col0, dst, dst_key):
                for hf in range(2):
                    ps, key = proj(col0 + hf * 512, 512, cur["UT"], cur["kUT"], WIN, 8)
                    tk = "TS%d" % hf
                    act(TS[hf], ps, AF.Exp, [key], [tk], scale=-1.0)
                    act(TS[hf], TS[hf], AF.Ln, [tk], [tk], bias=1.0)
                    act(TS[hf], TS[hf], AF.Exp, [tk], [tk], scale=-1.0)
                    tt("dve", dst[:, hf * 512:(hf + 1) * 512], ps, TS[hf], ALU.mult, [key, tk], [dst_key])

            def attention(t, kb):
                pb = kb ^ 1
                mask = MASK0 if t == 0 else MASK2
                mkey = "MASK0" if t == 0 else "MASK2"
                for sg in range(4):
                    g = sg // 2
                    obk = 7
                    okey = "B%d" % obk
                    for i in range(4):
                        h = sg * 4 + i
                        c, r0 = h // 2, (h % 2) * 64
                        sbk = (6, 4)[h % 2]
                        sk = "B%d" % sbk
                        sps = BK[sbk][:, 0:256]
                        pi = h % 4
                        mm(sps[:, 0:128], KTD[pb][r0:r0 + 64, g, :], QT[r0:r0 + 64, c * 128:(c + 1) * 128], True, True,
                           ["KTD%d" % pb, "QT"], [sk], inc=False)
                        mm(sps[:, 128:256], KTD[kb][r0:r0 + 64, g, :], QT[r0:r0 + 64, c * 128:(c + 1) * 128], True, True,
                           ["KTD%d" % kb, "QT"], [sk], inc=True)
                        pk = "PT%d" % pi
                        act(PT[pi][:].rearrange("p a q -> p (a q)"), sps, AF.Exp, [sk], [pk], scale=0.125)
                        tt("dve", PT[pi][:], PT[pi][:], mask[:], ALU.mult, [pk, mkey], [pk])
                        ops = BK[obk][:, i * 65:(i + 1) * 65]
                        mm(ops, PT[pi][:, 0, :], VA[pb][:, g, :], True, False, [pk, "VA%d" % pb], [okey], inc=False)
                        mm(ops, PT[pi][:, 1, :], VA[kb][:, g, :], False, True, [pk, "VA%d" % kb], [okey], inc=True)
                    o3 = BK[obk][:, 0:260].rearrange("p (h d) -> p h d", h=4)
                    DEN = ST[:, 20:24]
                    tt("dve", DEN, o3[:, :, 64], ESINK[:, sg * 4:(sg + 1) * 4], ALU.add, [okey, "ESINK"], ["DEN"])
                    S.op("dve", lambda e, o=DEN: e.reciprocal(out=o, in_=o), ["DEN"], ["DEN"], cost=0.2)
                    ysl = Y[:, sg * 256:(sg + 1) * 256]
                    tt("dve", ysl.rearrange("p (h d) -> p h d", h=4), o3[:, :, 0:64],
                       DEN.unsqueeze(2).broadcast_to([128, 4, 64]), ALU.mult, [okey, "DEN"], ["Ya%d" % sg])
                    tt("pool", ysl, ysl, SA[:, sg * 256:(sg + 1) * 256], ALU.mult, ["Ya%d" % sg, "SA"], ["Ya%d" % sg])

            def gla(t):
                for h in range(4):
                    abk = 6
                    ak = "B%d" % abk
                    aps = BK[abk][:, 256:384]
                    mm(aps, KIT[:, h * 128:(h + 1) * 128], QDT[:, h * 128:(h + 1) * 128], True, True, ["KIT", "QDT"], [ak])
                    AT = PT[h % 2][:, 0, :]
                    tt("dve", AT, aps, TRI[:], ALU.mult, [ak, "TRI"], ["PT%d" % (h % 2)])
                    obk = 7 if h % 2 == 0 else 5
                    ok = "B%d" % obk
                    ops = BK[obk][:, 0:256]
                    mm(ops, AT, VG[:, h, :], True, False, ["PT%d" % (h % 2), "VG"], [ok], inc=False)
                    mm(ops, QDT[:, h * 128:(h + 1) * 128], SB[:, h, :], False, True, ["QDT", "SB%d" % h], [ok], inc=True)
                    ysl = Y[:, 1024 + h * 256:1024 + (h + 1) * 256]
                    act(ysl, ops, AF.Square, [ok], ["Yg%d" % h, "ssg%d" % h], accum=ST[:, 24 + h:25 + h])
                    rstd_from_ss(ST[:, 24 + h:25 + h], ST[:, 28 + h:29 + h], 256, "ssg%d" % h, "rsg%d" % h)
                    stt("dve", ysl, ops, ST[:, 28 + h:29 + h], SG[:, h * 256:(h + 1) * 256], ALU.mult, ALU.mult,
                        [ok, "rsg%d" % h, "SG"], ["Yg%d" % h])
                    gla_kv_state(h, t)
                    cp("pool", SB[:, h, :], SST[:, h, :], ["SST%d" % h], ["SB%d" % h])

            def stage_F(t, xb):
                dma(PF[:], p_d[t * 128:(t + 1) * 128, :], "d_p", writes=["PF"])
                YT3w = YT.rearrange("p (c t) -> p c t", c=16)
                Y3 = Y[:].rearrange("p (c t) -> p c t", c=16)
                dmaT(YT3w[:, 0:8, :], Y3[:, 0:8, :], "d_ty0", ["Ya0", "Ya1", "Ya2", "Ya3"], ["YTa"])
                dmaT(YT3w[:, 8:16, :], Y3[:, 8:16, :], "d_ty1", ["Yg0", "Yg1", "Yg2", "Yg3"], ["YTb"])
                YT3 = YT.rearrange("p (c t) -> p c t", c=16)
                dma(H, x_d[t * 128:(t + 1) * 128, :], "d_xr", writes=["H"])
                for hf in range(2):
                    ps, key = proj(hf * 512, 512, YT3, ["YTa", "YTb"], WOUT, 16)
                    tt("dve", H[:, hf * 512:(hf + 1) * 512], ps, H[:, hf * 512:(hf + 1) * 512], ALU.add,
                       [key, "H"], ["H"])
                act(HN, H, AF.Square, ["H"], ["HN", "ss1"], accum=ST[:, 2:3])
                rstd_from_ss(ST[:, 2:3], ST[:, 3:4], D, "ss1", "rs1")
                tsc("dve", HN, H, ST[:, 3:4], ALU.mult, ["H", "rs1"], ["HN"])
                dmaT(HNT.rearrange("p (c t) -> p c t", c=8), HN.rearrange("p (c t) -> p c t", c=8), "d_th", ["HN"], ["HNT"])
                HNT3 = HNT.rearrange("p (c t) -> p c t", c=8)
                for hf in range(2):
                    ps, key = proj(hf * 512, 512, HNT3, "HNT", WPG, 8)
                    tk = "SIG%s" % "ab"[hf]
                    sg = SIG[:, hf * 512:(hf + 1) * 512]
                    act(sg, ps, AF.Exp, [key], [tk], scale=-1.0)
                    act(sg, sg, AF.Ln, [tk], [tk], bias=1.0)
                    act(sg, sg, AF.Exp, [tk], [tk], scale=-1.0)
                cp("pool", PB, PF[:], ["PF"], ["PB"])
                dmaT(PTT.rearrange("p (c t) -> p c t", c=2), PB.rearrange("p (c t) -> p c t", c=2), "d_tp", ["PB"], ["PTT"])
                PTT3 = PTT.rearrange("p (c t) -> p c t", c=2)
                for hf in range(2):
                    ps, key = proj(hf * 512, 512, PTT3, "PTT", WPP, 2)
                    tk = "SIG%s" % "ab"[hf]
                    tt("dve", SIG[:, hf * 512:(hf + 1) * 512], SIG[:, hf * 512:(hf + 1) * 512], ps, ALU.mult,
                       [tk, key], [tk])
                tt("dve", H, H, SIG, ALU.add, ["H", "SIG"], ["H"])
                act(OUT, H, AF.Square, ["H"], ["OUT", "ss2"], accum=ST[:, 4:5])
                rstd_from_ss(ST[:, 4:5], ST[:, 5:6], D, "ss2", "rs2")
                stt("dve", OUT, H, ST[:, 5:6], FN[:], ALU.mult, ALU.mult, ["H", "rs2", "FN"], ["OUT"])
                dma(out_d[t * 128:(t + 1) * 128, :], OUT, "d_o", reads=["OUT"], writes=["out"])

            if LVL >= 5:
                use_ut(1)
                kv_attn_proj(0, 1)

                def front(t):
                    use_ut(t % 2)
                    kb = t % 2
                    kv_attn_proj(t + 1, kb)
                    Qb, kQ = XNs[t % 2], "XN%d" % (t % 2)
                    for hf in range(2):
                        ps, key = proj(C_AQ + hf * 512, 512, cur["UT"], cur["kUT"], WIN, 8)
                        rope_evac(ps, key, 8, Qb[:, hf * 512:(hf + 1) * 512].rearrange("p (h d) -> p h d", h=8), kQ,
                                  t + 1, False)
                    dmaT(QT.rearrange("p (c t) -> p c t", c=8), Qb[:].rearrange("p (c t) -> p c t", c=8), "d_tq", [kQ], ["QT"])
                    silu_evac(C_ZA, SA[:], "SA")
                    if t + 1 < P2T:
                        stage_A(x_d[(t + 1) * 128:(t + 2) * 128, :], (t + 1) % 2)
                        if t + 2 < P2T:
                            load_x(x_d[(t + 2) * 128:(t + 3) * 128, :])

                def back(t):
                    use_ut(t % 2)
                    kb = t % 2
                    rows = slice(t * 128, (t + 1) * 128)
                    dma(E1, sc_e1[rows, :], "d_l1", writes=["E1"])
                    dmaT(KIT.rearrange("p (c t) -> p c t", c=4), sc_ki[rows, :].rearrange("p (c t) -> p c t", c=4),
                         "d_l2", [], ["KIT"])
                    dma(KTL, sc_ki[rows, :], "d_l3", writes=["KTL"])
                    dma(VG[:].rearrange("p h d -> p (h d)"), sc_vg[rows, :], "d_l4", writes=["VG"])
                    attention(t, kb)
                    silu_evac(C_ZG, SG[:], "SG")
                    ps, key = proj(C_GQ, 512, cur["UT"], cur["kUT"], WIN, 8)
                    stt("dve", QD, ps, float(128 ** -0.5), E1, ALU.mult, ALU.mult, [key, "E1"], ["QD"])
                    dmaT(QDT.rearrange("p (c t) -> p c t", c=4), QD.rearrange("p (c t) -> p c t", c=4), "d_tqd", ["QD"], ["QDT"])
                    gla(t)

                front(0)
                for t in range(P2T):
                    S.parity = t % 2
                    back(t)
                    if t + 1 < P2T:
                        front(t + 1)
                    stage_F(t, 0)

        try:
            _phases()
        except _Stop:
            pass

        S.barrier()

        return _emit(nc, S, es)


_NC_CACHE = {}


def _get_nc():
    if "nc" not in _NC_CACHE:
        _NC_CACHE["nc"] = build_nc()
    return _NC_CACHE["nc"]


def kernel(x, p, positions, norm_mix, w_in, attn_sinks, w_gate_up, b_gate, gla_norm,
           w_out, ple_norm, w_ple_gate, w_ple_proj, final_norm):
    f32 = np.float32
    x = np.asarray(x, f32)
    p = np.asarray(p, f32)
    positions = np.asarray(positions, np.int32)
    w_in0 = np.ascontiguousarray(np.asarray(w_in, f32)[0])
    w_out0 = np.ascontiguousarray(np.asarray(w_out, f32)[0])
    w_pg0 = np.ascontiguousarray(np.asarray(w_ple_gate, f32)[0])
    w_pp0 = np.ascontiguousarray(np.asarray(w_ple_proj, f32)[0])
    nm = np.ascontiguousarray(np.asarray(norm_mix, f32)[0].reshape(8, 128).T)
    pn = np.ascontiguousarray(np.asarray(ple_norm, f32)[0].reshape(8, 128).T)
    gn = np.ascontiguousarray(np.asarray(gla_norm, f32)[0].reshape(2, 128).T)
    fn = np.ascontiguousarray(np.broadcast_to(np.asarray(final_norm, f32)[None, :], (128, D)))
    sinks = np.ascontiguousarray(np.broadcast_to(np.asarray(attn_sinks, f32)[0][None, :], (128, 16)))
    wgu = np.ascontiguousarray(np.concatenate([np.asarray(w_gate_up, f32)[0], np.asarray(b_gate, f32)[0][None, :]], 0))
    ident = np.eye(128, dtype=f32)
    ii = np.arange(128)
    tri = (ii[:, None] <= ii[None, :]).astype(f32)
    tric = np.ascontiguousarray(1.0 - tri)
    invf = np.power(f32(500000.0), -(np.arange(0, 16, 2, dtype=f32) / f32(16))).astype(f32)
    invf = np.ascontiguousarray(np.broadcast_to(invf[None, :], (128, 8)))

    in_maps = []
    for c in range(NCORES):
        b, j = c // 4, c % 4
        t0 = j * TOK
        xs = np.ascontiguousarray(x[b, t0:t0 + TOK])
        if j == 0:
            xh = np.zeros((128, D), f32)
            ph = np.zeros((128,), np.int32)
        else:
            xh = np.ascontiguousarray(x[b, t0 - 128:t0])
            ph = positions[b, t0 - 128:t0]
        pos = np.concatenate([ph, positions[b, t0:t0 + TOK]]).reshape(17, 128).T
        flags = np.zeros((128, 3), f32)
        flags[:, :j] = 1.0
        in_maps.append({
            "x": xs, "xh": xh, "p": np.ascontiguousarray(p[0, b, t0:t0 + TOK]),
            "pos": np.ascontiguousarray(pos.astype(np.int32)),
            "w_in": w_in0, "w_out": w_out0, "w_pg": w_pg0, "w_pp": w_pp0,
            "nm": nm, "pn": pn, "gn": gn, "fn": fn, "sinks": sinks, "wgu": wgu,
            "ident": ident, "tri": tri, "tric": tric,
            "mprev0": tric if j > 0 else np.zeros((128, 128), f32),
            "flags": flags, "invf": invf,
        })
    nc = _get_nc()
    res = run_bass_kernel_spmd(nc, in_maps, core_ids=list(range(NCORES)))
    out = np.empty((2, 8192, D), f32)
    for c in range(NCORES):
        b, j = c // 4, c % 4
        out[b, j * TOK:(j + 1) * TOK] = np.asarray(res.results[c]["out"], f32)
    return out
```
